# Optimizing a Trainium2 kernel written in Bass

```python
import jax, jax.numpy as jnp
from jax import lax
import numpy as np


D_MODEL = 1024
BATCH = 2
SEQ = 16384
DEPTH = 4

N_A_LAYERS = DEPTH // 2
N_B_LAYERS = DEPTH - N_A_LAYERS
HGRN_EXPAND = 128
HGRN_HEADS = D_MODEL // HGRN_EXPAND
HGRN_DK = HGRN_EXPAND
HGRN_DV = D_MODEL // HGRN_HEADS
HGRN_D_KEY = HGRN_HEADS * HGRN_DK
HGRN_D_VAL = HGRN_HEADS * HGRN_DV
HGRN_CHUNK = 64
NSA_HEADS = 16
NSA_KV_HEADS = 4
NSA_GROUP = NSA_HEADS // NSA_KV_HEADS
NSA_HEAD_DIM = D_MODEL // NSA_HEADS
NSA_Q_WIDTH = NSA_HEADS * NSA_HEAD_DIM
N_BRANCH = 3
CMP_BLOCK = 32
CMP_STRIDE = 16
CMP_HIDDEN = 256
SEL_BLOCK = 64
N_SELECT = 16
WINDOW = 512
Q_BLOCK = 128
ROPE_THETA = 10000.0
FFN_HIDDEN = -(-8 * D_MODEL // (3 * 256)) * 256
EPS = 1e-6
NEG_INF = -1e30
FORCE_SCORE = 1e9

kernel_name = 'yoco_hgrn2_nsa_hybrid'


def rms_norm(x, gain):
    x32 = x.astype(jnp.float32)
    y = x32 * lax.rsqrt(jnp.mean(x32 * x32, axis=-1, keepdims=True) + EPS)
    return (y * gain.astype(jnp.float32)).astype(x.dtype)


def modulate(x, gain, shift, scale):
    return rms_norm(x, gain) * (1 + scale[:, None, :]) + shift[:, None, :]


def rope_tables(seq, dim):
    inv_freq = 1.0 / (ROPE_THETA ** (jnp.arange(0, dim, 2, dtype=jnp.float32) / dim))
    ang = jnp.arange(seq, dtype=jnp.float32)[:, None] * inv_freq[None, :]
    return jnp.cos(ang), jnp.sin(ang)


def apply_rope(x, cos, sin):
    x1, x2 = jnp.split(x, 2, axis=-1)
    cs = cos[None, :, None, :].astype(x.dtype)
    sn = sin[None, :, None, :].astype(x.dtype)
    return jnp.concatenate([x1 * cs - x2 * sn, x2 * cs + x1 * sn], axis=-1)


def masked_softmax(s, mask):
    p = jax.nn.softmax(jnp.where(mask, s.astype(jnp.float32), NEG_INF), axis=-1)
    return jnp.where(mask, p, 0.0)


def swiglu(h, w_in, w_out):
    a, b = jnp.split(h @ w_in, 2, axis=-1)
    return (jax.nn.silu(a) * b) @ w_out


def hgrn2_mixer(h, w_in, lower_bound, out_norm, w_out, first_layer):
    B, S, _ = h.shape
    q, f, i, g = jnp.split(h @ w_in, [HGRN_D_KEY, 2 * HGRN_D_KEY, 2 * HGRN_D_KEY + HGRN_D_VAL], axis=-1)
    q = jax.nn.silu(q)
    f32 = f.astype(jnp.float32)
    if first_layer:
        log_f = jax.nn.log_sigmoid(f32)
    else:
        lb = lower_bound.astype(jnp.float32)
        log_f = jnp.log(lb + (1.0 - lb) * jax.nn.sigmoid(f32))
    k = -jnp.expm1(log_f)
    nc = S // HGRN_CHUNK

    def to_chunks(t, d):
        return t.astype(jnp.float32).reshape(B, nc, HGRN_CHUNK, HGRN_HEADS, d).transpose(1, 0, 3, 2, 4)

    qc, kc, gc = to_chunks(q, HGRN_DK), to_chunks(k, HGRN_DK), to_chunks(log_f, HGRN_DK)
    vc = to_chunks(i, HGRN_DV)
    causal = jnp.tril(jnp.ones((HGRN_CHUNK, HGRN_CHUNK), dtype=bool))[:, :, None]

    def step(state, inp):
        q_, k_, v_, g_ = inp
        G = jnp.cumsum(g_, axis=2)
        o_inter = jnp.einsum('bhtk,bhkv->bhtv', q_ * jnp.exp(G), state)
        diff = G[:, :, :, None, :] - G[:, :, None, :, :]
        decay = jnp.where(causal, jnp.exp(jnp.where(causal, diff, 0.0)), 0.0)
        attn = jnp.einsum('bhtk,bhsk,bhtsk->bhts', q_, k_, decay)
        o_intra = jnp.einsum('bhts,bhsv->bhtv', attn, v_)
        G_last = G[:, :, -1:, :]
        new_state = jnp.exp(G_last[:, :, 0, :])[..., None] * state + jnp.einsum(
            'bhsk,bhsv->bhkv', k_ * jnp.exp(G_last - G), v_)
        return new_state, o_inter + o_intra

    s0 = jnp.zeros((B, HGRN_HEADS, HGRN_DK, HGRN_DV), jnp.float32)
    _, o = lax.scan(step, s0, (qc, kc, vc, gc))
    o = o.transpose(1, 0, 3, 2, 4).reshape(B, S, HGRN_HEADS, HGRN_DV)
    o = rms_norm(o, out_norm) * jax.nn.silu(g.reshape(B, S, HGRN_HEADS, HGRN_DV).astype(jnp.float32))
    return o.reshape(B, S, HGRN_D_VAL).astype(h.dtype) @ w_out


def nsa_shared_kv(h, w_kv, k_norm, cmp_pos, cmp_w1, cmp_w2, cos, sin):
    B, S, _ = h.shape
    kv = (h @ w_kv).reshape(B, S, 2 * N_BRANCH, NSA_KV_HEADS, NSA_HEAD_DIM)
    k_cmp, v_cmp, k_sel, v_sel, k_win, v_win = [kv[:, :, j] for j in range(2 * N_BRANCH)]
    n_sub = CMP_BLOCK // CMP_STRIDE
    n_cmp = S // CMP_STRIDE - n_sub + 1

    def compress(t, pos, w1, w2):
        r = t.reshape(B, S // CMP_STRIDE, CMP_STRIDE, NSA_KV_HEADS, NSA_HEAD_DIM)
        blocks = jnp.concatenate([r[:, j:j + n_cmp] for j in range(n_sub)], axis=2)
        blocks = blocks + pos[None, None, :, None, :]
        flat = blocks.transpose(0, 1, 3, 2, 4).reshape(B, n_cmp, NSA_KV_HEADS, CMP_BLOCK * NSA_HEAD_DIM)
        return jax.nn.silu(flat @ w1) @ w2

    kc = rms_norm(compress(k_cmp, cmp_pos[0], cmp_w1[0], cmp_w2[0]), k_norm[0])
    vc = compress(v_cmp, cmp_pos[1], cmp_w1[1], cmp_w2[1])
    n_sel = S // SEL_BLOCK
    ks = apply_rope(rms_norm(k_sel, k_norm[1]), cos, sin)
    ks_blocks = ks.reshape(B, n_sel, SEL_BLOCK, NSA_KV_HEADS, NSA_HEAD_DIM).transpose(0, 3, 1, 2, 4)
    vs_blocks = v_sel.reshape(B, n_sel, SEL_BLOCK, NSA_KV_HEADS, NSA_HEAD_DIM).transpose(0, 3, 1, 2, 4)
    kw = apply_rope(rms_norm(k_win, k_norm[2]), cos, sin)
    pad = ((0, 0), (WINDOW, 0), (0, 0), (0, 0))
    return (kc, vc, ks_blocks, vs_blocks, jnp.pad(kw, pad), jnp.pad(v_win, pad))


def nsa_mixer(h, w_q, q_norm, w_out, shared, cos, sin):
    B, S, _ = h.shape
    kc, vc, ks_blocks, vs_blocks, kw_pad, vw_pad = shared
    proj = h @ w_q
    q = rms_norm(proj[..., :NSA_Q_WIDTH].reshape(B, S, NSA_HEADS, NSA_HEAD_DIM), q_norm)
    gates = jax.nn.sigmoid(proj[..., NSA_Q_WIDTH:].astype(jnp.float32)).reshape(B, S, N_BRANCH, NSA_HEADS)
    q_rope = apply_rope(q, cos, sin)
    n_cmp = kc.shape[1]
    n_sel = ks_blocks.shape[2]
    n_top = min(N_SELECT, n_sel)
    cmp_end = jnp.arange(n_cmp) * CMP_STRIDE + CMP_BLOCK - 1
    sel_start = jnp.arange(n_sel) * SEL_BLOCK
    cs = np.arange(n_cmp) * CMP_STRIDE
    ss = np.arange(n_sel) * SEL_BLOCK
    overlap = jnp.asarray(((cs[:, None] < ss[None, :] + SEL_BLOCK) &
                           (cs[:, None] + CMP_BLOCK > ss[None, :])).astype(np.float32))
    scale = NSA_HEAD_DIM ** -0.5
    nb = S // Q_BLOCK
    bi = jnp.arange(B)[:, None, None, None]
    hi = jnp.arange(NSA_KV_HEADS)[None, :, None, None]
    blk_pos = jnp.arange(SEL_BLOCK)

    def blockify(t):
        return t.reshape(B, nb, Q_BLOCK, *t.shape[2:]).swapaxes(0, 1)

    def attend(args):
        blk, qp, qr, g = args
        t = blk * Q_BLOCK + jnp.arange(Q_BLOCK)
        qp = qp.reshape(B, Q_BLOCK, NSA_KV_HEADS, NSA_GROUP, NSA_HEAD_DIM)
        qr = qr.reshape(B, Q_BLOCK, NSA_KV_HEADS, NSA_GROUP, NSA_HEAD_DIM)
        s_c = jnp.einsum('bqhgd,bchd->bhgqc', qp, kc) * scale
        p_c = masked_softmax(s_c, cmp_end[None, :] <= t[:, None])
        o_c = jnp.einsum('bhgqc,bchd->bqhgd', p_c.astype(vc.dtype), vc)
        imp = jnp.einsum('bhgqc,cj->bhqj', p_c, overlap)
        cur = t // SEL_BLOCK
        j = jnp.arange(n_sel)[None, :]
        forced = (j == 0) | (j == cur[:, None]) | (j == cur[:, None] - 1)
        valid_blk = sel_start[None, :] <= t[:, None]
        imp = jnp.where(forced & valid_blk, FORCE_SCORE, jnp.where(valid_blk, imp, -1.0))
        top_val, top_idx = lax.top_k(imp, n_top)
        ks_g = ks_blocks[bi, hi, top_idx].reshape(B, NSA_KV_HEADS, Q_BLOCK, n_top * SEL_BLOCK, NSA_HEAD_DIM)
        vs_g = vs_blocks[bi, hi, top_idx].reshape(B, NSA_KV_HEADS, Q_BLOCK, n_top * SEL_BLOCK, NSA_HEAD_DIM)
        kpos = (top_idx[..., None] * SEL_BLOCK + blk_pos).reshape(B, NSA_KV_HEADS, Q_BLOCK, n_top * SEL_BLOCK)
        m_s = jnp.repeat(top_val >= 0, SEL_BLOCK, axis=-1) & (kpos <= t[None, None, :, None])
        s_s = jnp.einsum('bqhgd,bhqkd->bhgqk', qr, ks_g) * scale
        p_s = masked_softmax(s_s, m_s[:, :, None])
        o_s = jnp.einsum('bhgqk,bhqkd->bqhgd', p_s.astype(vs_g.dtype), vs_g)
        kw = lax.dynamic_slice_in_dim(kw_pad, blk * Q_BLOCK, Q_BLOCK + WINDOW, axis=1)
        vw = lax.dynamic_slice_in_dim(vw_pad, blk * Q_BLOCK, Q_BLOCK + WINDOW, axis=1)
        kpos_w = blk * Q_BLOCK - WINDOW + jnp.arange(Q_BLOCK + WINDOW)
        m_w = ((kpos_w[None, :] <= t[:, None]) & (kpos_w[None, :] > t[:, None] - WINDOW)
               & (kpos_w[None, :] >= 0))
        s_w = jnp.einsum('bqhgd,bkhd->bhgqk', qr, kw) * scale
        p_w = masked_softmax(s_w, m_w)
        o_w = jnp.einsum('bhgqk,bkhd->bqhgd', p_w.astype(vw.dtype), vw)
        g = g.reshape(B, Q_BLOCK, N_BRANCH, NSA_KV_HEADS, NSA_GROUP)[..., None]
        o = g[:, :, 0] * o_c + g[:, :, 1] * o_s + g[:, :, 2] * o_w
        return o.reshape(B, Q_BLOCK, NSA_Q_WIDTH).astype(h.dtype)

    out = lax.map(attend, (jnp.arange(nb), blockify(q), blockify(q_rope), blockify(gates)))
    return out.swapaxes(0, 1).reshape(B, S, NSA_Q_WIDTH) @ w_out


def setup_inputs(seed: int = 0) -> dict:
    key = jax.random.key(seed)
    ks = jax.random.split(key, 24)
    D = D_MODEL

    def nrm(k, shape, scale):
        return jax.random.normal(k, shape, jnp.float32) * scale

    def gain(k, shape):
        return 1.0 + nrm(k, shape, 0.02)

    return {
        'x': nrm(ks[0], (BATCH, SEQ, D), 1.0),
        'c': nrm(ks[1], (BATCH, D), 1.0),
        'ada_w': nrm(ks[2], (2 * DEPTH, D, 3 * D), 0.5 * D ** -0.5),
        'ada_b': nrm(ks[3], (2 * DEPTH, 3 * D), 0.02),
        'norm_mix': gain(ks[4], (DEPTH, D)),
        'norm_ffn': gain(ks[5], (DEPTH, D)),
        'hgrn_w_in': nrm(ks[6], (N_A_LAYERS, D, 2 * HGRN_D_KEY + 2 * HGRN_D_VAL), D ** -0.5),
        'hgrn_lower_bounds': nrm(ks[7], (N_A_LAYERS, HGRN_D_KEY), 0.5),
        'hgrn_out_norm': gain(ks[8], (N_A_LAYERS, HGRN_DV)),
        'hgrn_w_out': nrm(ks[9], (N_A_LAYERS, HGRN_D_VAL, D), HGRN_D_VAL ** -0.5),
        'kv_ada_w': nrm(ks[10], (D, 2 * D), 0.5 * D ** -0.5),
        'kv_ada_b': nrm(ks[11], (2 * D,), 0.02),
        'kv_norm': gain(ks[12], (D,)),
        'nsa_w_kv': nrm(ks[13], (D, 2 * N_BRANCH * NSA_KV_HEADS * NSA_HEAD_DIM), D ** -0.5),
        'nsa_k_norm': gain(ks[14], (N_BRANCH, NSA_HEAD_DIM)),
        'cmp_pos': nrm(ks[15], (2, CMP_BLOCK, NSA_HEAD_DIM), 0.1),
        'cmp_w1': nrm(ks[16], (2, CMP_BLOCK * NSA_HEAD_DIM, CMP_HIDDEN), (CMP_BLOCK * NSA_HEAD_DIM) ** -0.5),
        'cmp_w2': nrm(ks[17], (2, CMP_HIDDEN, NSA_HEAD_DIM), CMP_HIDDEN ** -0.5),
        'nsa_w_q': nrm(ks[18], (N_B_LAYERS, D, NSA_Q_WIDTH + N_BRANCH * NSA_HEADS), D ** -0.5),
        'nsa_q_norm': gain(ks[19], (N_B_LAYERS, NSA_HEAD_DIM)),
        'nsa_w_out': nrm(ks[20], (N_B_LAYERS, NSA_Q_WIDTH, D), NSA_Q_WIDTH ** -0.5),
        'ffn_w_in': nrm(ks[21], (DEPTH, D, 2 * FFN_HIDDEN), D ** -0.5),
        'ffn_w_out': nrm(ks[22], (DEPTH, FFN_HIDDEN, D), FFN_HIDDEN ** -0.5),
    }


def reference(x, c, ada_w, ada_b, norm_mix, norm_ffn, hgrn_w_in, hgrn_lower_bounds, hgrn_out_norm,
              hgrn_w_out, kv_ada_w, kv_ada_b, kv_norm, nsa_w_kv, nsa_k_norm, cmp_pos, cmp_w1, cmp_w2,
              nsa_w_q, nsa_q_norm, nsa_w_out, ffn_w_in, ffn_w_out):
    B, S, _ = x.shape
    cos, sin = rope_tables(S, NSA_HEAD_DIM)
    c_act = jax.nn.silu(c)
    mods = jnp.einsum('bd,lde->lbe', c_act, ada_w) + ada_b[:, None, :]
    lb = jax.nn.softmax(hgrn_lower_bounds.astype(jnp.float32), axis=0)
    lb = jnp.cumsum(lb, axis=0) - lb[0]
    shared = None
    for layer in range(DEPTH):
        shift, scale, gate = jnp.split(mods[2 * layer], 3, axis=-1)
        h = modulate(x, norm_mix[layer], shift, scale)
        if layer < N_A_LAYERS:
            y = hgrn2_mixer(h, hgrn_w_in[layer], lb[layer], hgrn_out_norm[layer], hgrn_w_out[layer], layer == 0)
        else:
            if layer == N_A_LAYERS:
                kv_shift, kv_scale = jnp.split(c_act @ kv_ada_w + kv_ada_b, 2, axis=-1)
                shared = nsa_shared_kv(modulate(x, kv_norm, kv_shift, kv_scale), nsa_w_kv, nsa_k_norm,
                                       cmp_pos, cmp_w1, cmp_w2, cos, sin)
            bl = layer - N_A_LAYERS
            y = nsa_mixer(h, nsa_w_q[bl], nsa_q_norm[bl], nsa_w_out[bl], shared, cos, sin)
        x = x + gate[:, None, :] * y
        shift, scale, gate = jnp.split(mods[2 * layer + 1], 3, axis=-1)
        h = modulate(x, norm_ffn[layer], shift, scale)
        x = x + gate[:, None, :] * swiglu(h, ffn_w_in[layer], ffn_w_out[layer])
    return x
```

```python
import numpy as np
import ml_dtypes
from contextlib import ExitStack
import concourse.bass as bass
import concourse.mybir as mybir
from concourse.bass_utils import run_bass_kernel_spmd

F32 = mybir.dt.float32
BF16 = mybir.dt.bfloat16
AF = mybir.ActivationFunctionType
ALU = mybir.AluOpType
NPBF = ml_dtypes.bfloat16

ENGS = ["pe", "act", "dve", "pool", "sp"]
EPS = 1e-6


class Op:
    __slots__ = ("eng", "fn", "deps", "idx", "dma", "sem", "inc", "val", "incidx", "ph")


class Prog:
    ARENA_BYTES = 206 * 1024
    N_DSEM = 36
    N_CSET = 12

    def __init__(self):
        self.nc = bass.Bass("TRN2", target_bir_lowering=False)
        self.st = ExitStack()
        self.ops = {e: [] for e in ENGS}
        self.last_w = {}
        self.readers = {}
        self.stores = []
        self.pending_dma = []
        self.arena = self.st.enter_context(self.nc.sbuf_tensor("arena", [128, self.ARENA_BYTES // 2], BF16))
        self.psum = self.st.enter_context(self.nc.psum_tensor("psum", [128, 8 * 512], F32))
        self.top = 0
        self.mark = 0
        self.sem_of_key = {}
        self.sem_cnt = [0] * self.N_DSEM
        self.sem_free = list(range(self.N_DSEM))
        self.phase = 0

    def din(self, name, shape, dt=F32):
        return self.nc.dram_tensor(name, list(shape), dt, kind="ExternalInput").ap()

    def dout(self, name, shape, dt=F32):
        return self.nc.dram_tensor(name, list(shape), dt, kind="ExternalOutput").ap()

    def dscratch(self, name, shape, dt=F32):
        return self.nc.dram_tensor(name, list(shape), dt, kind="Internal").ap()

    def alloc(self, shape, dt=F32):
        esz = 4 if dt == F32 else 2
        n = 1
        for d in shape:
            n *= d
        nb = (n * esz + 31) // 32 * 32
        assert self.top + nb <= self.ARENA_BYTES, ("SBUF arena overflow", self.top, nb)
        a = self.arena[:, self.top // 2:(self.top + nb) // 2]
        self.top += nb
        if dt == F32:
            a = a.bitcast(F32)
        a = a[:, 0:n]
        if len(shape) == 2:
            a = a.rearrange("p (a b) -> p a b", a=shape[0])
        elif len(shape) == 3:
            a = a.rearrange("p (a b c) -> p a b c", a=shape[0], b=shape[1])
        return a

    def bank(self, i, dt=F32):
        b = self.psum[:, i * 512:(i + 1) * 512]
        if dt == BF16:
            b = b.bitcast(BF16)
        return b

    def set_mark(self):
        self.mark = self.top

    def new_phase(self):
        self.barrier()
        self.top = self.mark

    def _sem_for(self, key):
        s = self.sem_of_key.get(key)
        if s is None:
            s = self.sem_free.pop(0)
            self.sem_of_key[key] = s
        return s

    def op(self, eng, fn, r=(), w=(), dma=False, semkey=None):
        o = Op()
        o.eng, o.fn, o.dma, o.inc, o.val, o.incidx, o.sem = eng, fn, dma, False, 0, 0, None
        o.ph = self.phase
        if dma:
            sk = semkey if semkey is not None else (w[0] if w else r[0])
            o.sem = self._sem_for(sk)
            self.sem_cnt[o.sem] += 16
            o.val = self.sem_cnt[o.sem]
            self.pending_dma.append(o)
        deps = []
        for k in r:
            lw = self.last_w.get(k)
            if lw is not None:
                deps.append(lw)
        for k in w:
            lw = self.last_w.get(k)
            if lw is not None and not (dma and lw.dma and lw.sem == o.sem):
                deps.append(lw)
            deps.extend(self.readers.get(k, ()))
        o.deps = [d for d in deps if d is not o]
        for k in r:
            self.readers.setdefault(k, []).append(o)
        for k in w:
            self.last_w[k] = o
            self.readers[k] = []
        o.idx = len(self.ops[eng])
        self.ops[eng].append(o)
        return o

    def barrier(self):
        tails = [self.ops[e][-1] for e in ENGS if self.ops[e]] + list(self.pending_dma)
        for e in ENGS:
            o = Op()
            o.eng, o.fn, o.dma, o.inc, o.val, o.incidx, o.sem = e, None, False, False, 0, 0, None
            o.ph = self.phase
            o.deps = [d for d in tails]
            o.idx = len(self.ops[e])
            self.ops[e].append(o)
        self.pending_dma = []
        self.last_w = {}
        self.readers = {}
        self.phase += 1
        for k, s in self.sem_of_key.items():
            self.sem_free.append(s)
        self.sem_of_key = {}

    def mm(self, out, lhsT, rhs, start, stop, r, w):
        return self.op("pe", lambda e: e.matmul(out, lhsT, rhs, start=start, stop=stop), r=r, w=w)

    def tr(self, out, in_, ident, r, w):
        return self.op("pe", lambda e: e.transpose(out, in_, ident), r=r, w=w)

    def act(self, out, in_, func, r, w, bias=None, scale=None):
        kw = {}
        if bias is not None:
            kw["bias"] = bias
        if scale is not None:
            kw["scale"] = scale
        return self.op("act", lambda e: e.activation(out, in_, func, **kw), r=r, w=w)

    def tt(self, out, a, b, op, r, w, eng="dve"):
        return self.op(eng, lambda e: e.tensor_tensor(out, a, b, op), r=r, w=w)

    def ts(self, out, a, s1, s2, op0, op1, r, w, eng="dve"):
        if op1 is None:
            return self.op(eng, lambda e: e.tensor_scalar(out, a, s1, None, op0), r=r, w=w)
        return self.op(eng, lambda e: e.tensor_scalar(out, a, s1, s2, op0, op1), r=r, w=w)

    def stt(self, out, in0, scalar, in1, op0, op1, r, w, eng="dve"):
        return self.op(eng, lambda e: e.scalar_tensor_tensor(out, in0, scalar, in1, op0, op1), r=r, w=w)

    def copy(self, out, in_, r, w, eng="dve"):
        if eng == "act":
            return self.op("act", lambda e: e.copy(out, in_), r=r, w=w)
        return self.op(eng, lambda e: e.tensor_copy(out, in_), r=r, w=w)

    def memset(self, ap, val, w, eng="pool"):
        return self.op(eng, lambda e: e.memset(ap, val), r=(), w=w)

    def load(self, eng, out, in_, w, r=(), semkey=None):
        return self.op(eng, lambda e: e.dma_start(out=out, in_=in_), r=r, w=w, dma=True, semkey=semkey)

    def store(self, eng, out, in_, r, w=(), semkey=None, final=True):
        o = self.op(eng, lambda e: e.dma_start(out=out, in_=in_), r=r, w=w, dma=True, semkey=semkey)
        if final:
            self.stores.append(o)
        return o

    def _plan(self):
        plan = {}
        for e in ENGS:
            seen_c = {f: -1 for f in ENGS}
            seen_d = {}
            lst = []
            for o in self.ops[e]:
                cmax = {}
                dmax = {}
                for d in o.deps:
                    if d.dma:
                        if d.val > dmax.get(d.sem, (0, None))[0]:
                            dmax[d.sem] = (d.val, d)
                    else:
                        if d.fn is None:
                            continue
                        if d.eng == e and e == "pe":
                            continue
                        if d.idx > cmax.get(d.eng, (-1, None))[0]:
                            cmax[d.eng] = (d.idx, d)
                waits = []
                for f, (idx, d) in cmax.items():
                    if seen_c[f] < idx:
                        seen_c[f] = idx
                        d.inc = True
                        waits.append(d)
                for sk, (val, d) in dmax.items():
                    if seen_d.get(sk, 0) < val:
                        seen_d[sk] = val
                        waits.append(d)
                lst.append((o, waits))
            plan[e] = lst
        return plan

    def build(self):
        nc = self.nc
        self.barrier()
        plan = self._plan()
        self.n_instr = {e: len(self.ops[e]) for e in ENGS}
        for e in ENGS:
            cnt = [0] * self.N_CSET
            for o in self.ops[e]:
                if o.inc and not o.dma:
                    cnt[o.ph % self.N_CSET] += 1
                    o.incidx = cnt[o.ph % self.N_CSET]
        csem = {e: [self.st.enter_context(nc.semaphore("c_%s%d" % (e, i))) for i in range(self.N_CSET)]
                for e in ENGS}
        dsem = [self.st.enter_context(nc.semaphore("d%d" % i)) for i in range(self.N_DSEM)]

        def emit(ename, E):
            for o, waits in plan[ename]:
                for d in waits:
                    if d.dma:
                        E.wait_ge(dsem[d.sem], d.val)
                    else:
                        E.wait_ge(csem[d.eng][d.ph % self.N_CSET], d.incidx)
                if o.fn is None:
                    continue
                ins = o.fn(E)
                if o.dma:
                    ins.then_inc(dsem[o.sem], 16)
                elif o.inc:
                    ins.then_inc(csem[ename][o.ph % self.N_CSET], 1)

        with nc.Block() as block:
            @block.tensor
            def _(E):
                emit("pe", E)

            @block.scalar
            def _(E):
                emit("act", E)

            @block.vector
            def _(E):
                emit("dve", E)

            @block.gpsimd
            def _(E):
                emit("pool", E)

            @block.sync
            def _(E):
                emit("sp", E)
        self.st.close()
        return nc


def run_prog(prog_nc, in_maps):
    if DBG.get("one_core"):
        r = run_bass_kernel_spmd(prog_nc, [in_maps[0]], core_ids=[0]).results
        return [r[0]] * len(in_maps)
    res = run_bass_kernel_spmd(prog_nc, in_maps, core_ids=list(range(8)))
    return res.results


D = 1024
KT = 8
S_LEN = 16384
TC = 4096
TT = 512
NTT = TC // TT
FH = 2816
FJ = FH // 128
NMOD = 8 * 24 + 16
DBG = {}


def fm(v):
    v = np.asarray(v)
    return np.ascontiguousarray(v.reshape(-1, 128).T)


class Consts:
    pass


def setup_consts(P, cin):
    C = Consts()
    C.ones_d = P.alloc([128], BF16)
    C.ones_128 = P.alloc([128], BF16)
    C.eps = P.alloc([1])
    C.one = P.alloc([1])
    P.memset(C.ones_d, 1.0 / D, w=["c_ones_d"])
    P.memset(C.ones_128, 1.0 / 128, w=["c_ones_128"])
    P.memset(C.eps, EPS, w=["c_eps"])
    P.memset(C.one, 1.0, w=["c_one"])
    return C


def load_cast_weight(P, w_dram, w_sb, rows_kt, ncols, stage, key, chunk, eng="sp"):
    i = 0
    for kt in range(rows_kt):
        for c0 in range(0, ncols, chunk):
            c1 = min(ncols, c0 + chunk)
            sg = stage[i % len(stage)]
            skey = ("stage", i % len(stage))
            P.load(eng, sg[:, 0:c1 - c0], w_dram[kt * 128:(kt + 1) * 128, c0:c1], w=[skey])
            P.copy(w_sb[:, kt, c0:c1], sg[:, 0:c1 - c0], r=[skey], w=[key], eng="pool")
            i += 1


def rms_rstd(P, C, src, src_keys, nk, sq, sq_keys, ones, ones_key, ps_ss, ps_key, rstd, rstd_key, ncol):
    P.act(sq[:, 0:nk, 0:ncol], src[:, 0:nk, 0:ncol], AF.Square, r=list(src_keys), w=list(sq_keys))
    for k in range(nk):
        P.mm(ps_ss[:, 0:ncol], ones, sq[:, k, 0:ncol], k == 0, k == nk - 1,
             r=[sq_keys[k], ones_key], w=[ps_key])
    P.act(rstd[:, 0:ncol], ps_ss[:, 0:ncol], AF.Sqrt, r=[ps_key, "c_eps"], w=[rstd_key], bias=C.eps[:, 0:1], scale=1.0)
    P.op("dve", lambda e: e.reciprocal(rstd[:, 0:ncol], rstd[:, 0:ncol]), r=[rstd_key], w=[rstd_key])


def modulate(P, C, xt, xkeys, hT, hkeys, gs, gs_key, shift, shift_key, ps_ss, ps_key, rstd, tmp, ncol=TT):
    rms_rstd(P, C, xt, xkeys, KT, hT, hkeys, C.ones_d, "c_ones_d", ps_ss, ps_key, rstd, "rstd", ncol)
    for k in range(KT):
        t_ = tmp[k % 2]
        tk = ("tmp", k % 2)
        P.stt(t_[:, 0:ncol], xt[:, k, 0:ncol], gs[:, k:k + 1], rstd[:, 0:ncol], ALU.mult, ALU.mult,
              r=[xkeys[k], gs_key, "rstd"], w=[tk])
        P.act(hT[:, k, 0:ncol], t_[:, 0:ncol], AF.Identity, r=[tk, shift_key], w=[hkeys[k]],
              bias=shift[:, k:k + 1], scale=1.0)


def make_gs(P, gs, gs_key, gain, gain_key, scale, scale_key):
    P.stt(gs, scale, 1.0, gain, ALU.add, ALU.mult, r=[gain_key, scale_key], w=[gs_key])


def ph_mods(P, T, mods_out):
    c_sb = P.alloc([8])
    cact2 = P.alloc([8, 2])
    bias = P.alloc([52])
    wb = [P.alloc([8, 768]), P.alloc([8, 768])]
    P.load("sp", c_sb, T["cT"], w=["c_sb"])
    P.load("sp", bias, T["mod_bT"], w=["mbias"])
    P.act(cact2[:, :, 0], c_sb, AF.Silu, r=["c_sb"], w=["cact2"])
    P.act(cact2[:, :, 1], c_sb, AF.Silu, r=["c_sb"], w=["cact2"])
    for s in range(9):
        cw = 768 if s < 8 else 512
        wsrc = T["ada_w"][s] if s < 8 else T["kv_ada_w"]
        wv = wsrc.rearrange("(k p) n -> p k n", p=128)
        ps = P.bank(s % 2)
        pk = ("mps", s % 2)
        buf = wb[s % 2]
        bkey = ("wb", s % 2)
        P.load("sp" if s % 2 == 0 else "act", buf[:, :, 0:cw], wv[:, :, 0:cw], w=[bkey])
        nt = cw // 128
        for n in range(nt):
            for k in range(8):
                P.mm(ps[:, 2 * n:2 * n + 2], buf[:, k, n * 128:(n + 1) * 128], cact2[:, k, :], k == 0, k == 7,
                     r=[bkey, "cact2"], w=[pk])
        off = s * 6
        P.tt(mods_out[:, off:off + nt], ps[:, 0:2 * nt].rearrange("p (n two) -> p n two", two=2)[:, :, 0],
             bias[:, off:off + nt], ALU.add, r=[pk, "mbias"], w=["mods_out"])


def ph_hrec(P, C, T, modsT, layer):
    first = (layer == 0)
    Win = P.alloc([8, 4096], BF16)
    IU = P.alloc([256])
    Lrev = P.alloc([128])
    maskneg = P.alloc([128], BF16)
    rowmask = P.alloc([4])
    gain = P.alloc([8])
    gs = P.alloc([8])
    Sst = [P.alloc([8, 128]) for _ in range(4)]
    Sbf = [P.alloc([8, 128], BF16) for _ in range(4)]
    egp = P.alloc([8])
    egpm = P.alloc([2, 4])
    lbrow = None
    if not first:
        lbrow = P.alloc([1024])
    mark0 = P.top
    stage = [P.alloc([2048]), P.alloc([2048])]
    load_cast_weight(P, T["w_in"], Win, 8, 4096, stage, "Win", 2048)
    P.load("act", IU, T["cIU"], w=["IU"])
    P.load("act", Lrev, T["cLrev"], w=["Lrev"])
    P.load("act", maskneg, T["cmaskneg"], w=["maskneg"])
    P.load("act", rowmask, T["crowmask"], w=["rowmask"])
    P.load("act", gain, T["gainT"], w=["gain"])
    sh_off = 2 * layer * 24
    shift = modsT[:, sh_off:sh_off + 8]
    make_gs(P, gs, "gs", gain, "gain", modsT[:, sh_off + 8:sh_off + 16], "modsT")
    if not first:
        lbr = P.alloc([2, 1024])
        P.load("act", lbr, T["lbraw"], w=["lbr"])
        P.tt(lbrow, lbr[:, 1, :], lbr[:, 0, :], ALU.subtract, r=["lbr"], w=["lbrow"])
        P.act(lbrow, lbrow, AF.Sigmoid, r=["lbrow"], w=["lbrow"])
    P.memset(Sst[0], 0.0, w=["S0_%d" % h for h in range(8)])
    P.memset(Sbf[0], 0.0, w=["Sbf0_%d" % h for h in range(8)])
    P.memset(egp, 1.0, w=["egp%d" % h for h in range(8)])
    P.barrier()
    P.top = mark0
    xt = P.alloc([8, TT])
    xtb = xt.rearrange("p a b -> p (a b)").bitcast(BF16)
    oloc_sb = xtb[:, 0:4096].rearrange("p (a b) -> p a b", a=8)
    qseg_sb = xtb[:, 4096:8192].rearrange("p (a b) -> p a b", a=8)
    hT = P.alloc([8, TT], BF16)
    rstd = P.alloc([TT])
    tmp = [P.alloc([TT]), P.alloc([TT])]
    q_sb = P.alloc([8, TT], BF16)
    sg_sb = P.alloc([8, TT], BF16)
    f_sb = [P.alloc([1024]), P.alloc([1024])]
    v_sb = [P.alloc([1024], BF16), P.alloc([1024], BF16)]
    tA = P.alloc([1024])
    tB = P.alloc([1024])
    tCc = P.alloc([1024])
    kdc = [P.alloc([1024], BF16) for _ in range(4)]
    eG = [P.alloc([128]), P.alloc([128])]
    einv = [P.alloc([128]), P.alloc([128])]
    fT = [P.alloc([128]), P.alloc([128])]
    qg = [P.alloc([128], BF16), P.alloc([128], BF16)]
    kinv = [P.alloc([128], BF16), P.alloc([128], BF16)]
    attnT = [P.alloc([128], BF16), P.alloc([128], BF16)]
    xv = T["xT"].rearrange("(k p) t -> p k t", p=128)
    olv = T["oloc"].rearrange("(k p) t -> p k t", p=128)
    qsv = T["qseg"].rearrange("(k p) t -> p k t", p=128)
    sgv = T["sgT"].rearrange("(k p) t -> p k t", p=128)
    xkeys = ["xt%d" % k for k in range(8)]
    hkeys = [("hT", k) for k in range(8)]
    for tt in range(DBG.get("ntt", NTT)):
        c0 = tt * TT
        P.load("sp", xt, xv[:, :, c0:c0 + TT], w=xkeys)
        modulate(P, C, xt, xkeys, hT, hkeys, gs, "gs", shift, "modsT", P.bank(0), ("b", 0), rstd, tmp)
        for n in range(8):
            b = n % 2
            for k in range(8):
                P.mm(P.bank(b), Win[:, k, n * 128:(n + 1) * 128], hT[:, k, :], k == 0, k == 7,
                     r=["Win", hkeys[k]], w=[("b", b)])
            P.act(q_sb[:, n, :], P.bank(b), AF.Silu, r=[("b", b)], w=[("q", n)])
        for n in range(8):
            b = n % 2
            for k in range(8):
                P.mm(P.bank(b), Win[:, k, 3072 + n * 128:3072 + (n + 1) * 128], hT[:, k, :], k == 0, k == 7,
                     r=["Win", hkeys[k]], w=[("b", b)])
            P.act(sg_sb[:, n, :], P.bank(b), AF.Silu, r=[("b", b)], w=["sg_sb"])
        P.store("sp", sgv[:, :, c0:c0 + TT], sg_sb, r=["sg_sb"])
        for sub in range(DBG.get("nsub", 4)):
            sp_ = sub % 2
            fs, vs = f_sb[sp_], v_sb[sp_]
            fk, vk = ("f", sp_), ("v", sp_)
            tok = slice(sub * 128, (sub + 1) * 128)
            for half in range(2):
                b = 2 + half
                for k in range(8):
                    P.mm(P.bank(b), hT[:, k, tok], Win[:, k, 1024 + half * 512:1024 + (half + 1) * 512],
                         k == 0, k == 7, r=["Win", hkeys[k]], w=[("b", b)])
                P.copy(fs[:, half * 512:(half + 1) * 512], P.bank(b), r=[("b", b)], w=[fk], eng="dve")
            for half in range(2):
                b = 2 + half
                for k in range(8):
                    P.mm(P.bank(b), hT[:, k, tok], Win[:, k, 2048 + half * 512:2048 + (half + 1) * 512],
                         k == 0, k == 7, r=["Win", hkeys[k]], w=[("b", b)])
                P.copy(vs[:, half * 512:(half + 1) * 512], P.bank(b), r=[("b", b)], w=[vk], eng="act")
            P.act(tA, fs, AF.Exp, r=[fk], w=["tA"], scale=-1.0)
            P.act(tB, tA, AF.Ln, r=["tA", "c_one"], w=["tB"], bias=C.one[:, 0:1], scale=1.0)
            if first:
                P.ts(tB, tB, -1.0, None, ALU.mult, None, r=["tB"], w=["tB"])
            else:
                P.tt(tCc, tA, lbrow, ALU.mult, r=["tA", "lbrow"], w=["tC"])
                P.act(tCc, tCc, AF.Ln, r=["tC", "c_one"], w=["tC"], bias=C.one[:, 0:1], scale=1.0)
                P.tt(tB, tCc, tB, ALU.subtract, r=["tC", "tB"], w=["tB"])
            P.act(tCc, tB, AF.Exp, r=["tB"], w=["tC"])
            P.ts(tCc, tCc, -1.0, 1.0, ALU.mult, ALU.add, r=["tC"], w=["tC"])
            for half in range(2):
                b = 2 + half
                P.mm(P.bank(b), Lrev, tB[:, half * 512:(half + 1) * 512], True, True,
                     r=["Lrev", "tB"], w=[("b", b)])
                P.act(tA[:, half * 512:(half + 1) * 512], P.bank(b), AF.Exp, r=[("b", b)], w=["tA"])
            P.tt(tCc, tCc, tA, ALU.mult, r=["tC", "tA"], w=["tC"])
            for c in range(4):
                P.ts(kdc[c], tCc, rowmask[:, c:c + 1], None, ALU.mult, None, r=["tC", "rowmask"], w=[("kdc", c)])
            for h in range(DBG.get("nh", 8)):
                p = h % 2
                hc = slice(h * 128, (h + 1) * 128)
                bA = P.bank(4 + p)
                bB = P.bank(6 + p)
                kG, kA, kO = ("psG", p), ("psA", p), ("psO", p)
                P.mm(bA[:, 0:256], tB[:, hc], IU, True, True, r=["tB", "IU"], w=[kG])
                P.act(eG[p], bA[:, 128:256], AF.Exp, r=[kG], w=[("eG", p)])
                P.act(einv[p], bA[:, 128:256], AF.Exp, r=[kG], w=[("einv", p)], scale=-1.0)
                P.act(fT[p], bA[:, 0:128], AF.Exp, r=[kG], w=[("fT", p)])
                P.tt(qg[p], q_sb[:, h, tok], eG[p], ALU.mult, r=[("q", h), ("eG", p)], w=[("qg", p)])
                P.stt(kinv[p], fT[p], 1.0, einv[p], ALU.subtract, ALU.mult,
                      r=[("fT", p), ("einv", p)], w=[("kinv", p)])
                ek = "egp%d" % h
                cur, curk = egp[:, h:h + 1], ek
                for c in range(4):
                    cs = slice(sub * 128 + 32 * c, sub * 128 + 32 * c + 32)
                    P.stt(qseg_sb[:, h, cs], eG[p][:, 32 * c:32 * c + 32], cur, q_sb[:, h, cs], ALU.mult, ALU.mult,
                          r=[("eG", p), curk, ("q", h)], w=["qseg_sb"] + xkeys)
                    if c < 3:
                        nxt, nxtk = egpm[:, p, c:c + 1], ("egpm", p, c)
                    else:
                        nxt, nxtk = egp[:, h:h + 1], ek
                    P.tt(nxt, cur, eG[p][:, 32 * c + 31:32 * c + 32], ALU.mult, r=[curk, ("eG", p)], w=[nxtk])
                    cur, curk = nxt, nxtk
                P.mm(bA[:, 256:384], kinv[p], qg[p], True, True, r=[("kinv", p), ("qg", p)], w=[kA])
                P.tt(attnT[p], bA[:, 256:384], maskneg, ALU.mult, r=[kA, "maskneg"], w=[("attnT", p)])
                for c in range(4):
                    P.mm(bB[:, 128 * c:128 * c + 128], kdc[c][:, hc], vs[:, hc], True, True,
                         r=[("kdc", c), vk], w=[("psS", p, c)])
                for c in range(4):
                    src, srck = Sst[c], "S%d_%d" % (c, h)
                    dst = Sst[(c + 1) % 4]
                    dstk = "S%d_%d" % ((c + 1) % 4, h)
                    P.stt(dst[:, h, :], src[:, h, :], eG[p][:, 32 * c + 31:32 * c + 32], bB[:, 128 * c:128 * c + 128],
                          ALU.mult, ALU.add, r=[srck, ("eG", p), ("psS", p, c)], w=[dstk])
                    if c < 3:
                        P.copy(Sbf[c + 1][:, h, :], dst[:, h, :], r=[dstk], w=["Sbf%d_%d" % (c + 1, h)], eng="act")
                P.mm(bA[:, 384:512], vs[:, hc], attnT[p], True, False, r=[vk, ("attnT", p)], w=[kO])
                for c in range(4):
                    P.mm(bA[:, 384 + 32 * c:384 + 32 * c + 32], Sbf[c][:, h, :], qg[p][:, 32 * c:32 * c + 32],
                         False, c == 3, r=["Sbf%d_%d" % (c, h), ("qg", p)], w=[kO])
                P.copy(Sbf[0][:, h, :], Sst[0][:, h, :], r=["S0_%d" % h], w=["Sbf0_%d" % h], eng="act")
                P.copy(oloc_sb[:, h, tok], bA[:, 384:512], r=[kO], w=["oloc_sb"] + xkeys, eng="act")
        P.store("sp", olv[:, :, c0:c0 + TT], oloc_sb, r=["oloc_sb"] + xkeys)
        P.store("sp", qsv[:, :, c0:c0 + TT], qseg_sb, r=["qseg_sb"] + xkeys)
    P.store("sp", T["Sloc"], Sst[0], r=["S0_%d" % h for h in range(8)])
    P.store("sp", T["Aseg"], egp, r=["egp%d" % h for h in range(8)])


def ph_oproj(P, C, T, modsT, layer, hgrn):
    Wo = P.alloc([8, 1024], BF16)
    gate = modsT[:, 2 * layer * 24 + 16:2 * layer * 24 + 24]
    if hgrn:
        Sin = P.alloc([8, 128])
        Sin_bf = P.alloc([8, 128], BF16)
        onorm = P.alloc([1])
        oneh = P.alloc([4])
    mark0 = P.top
    stage = [P.alloc([2048]), P.alloc([2048])]
    load_cast_weight(P, T["w_out"], Wo, 8, 1024, stage, "Wo", 1024)
    if hgrn:
        Sall = P.alloc([4, 8, 128])
        Aall = P.alloc([4, 8])
        Tt = P.alloc([8, 128])
        P.load("act", Sall, T["Sall"], w=["Sall"])
        P.load("act", Aall, T["Aall"], w=["Aall"])
        P.load("act", oneh, T["onehot"], w=["oneh"])
        P.load("act", onorm, T["onormT"], w=["onorm"])
        P.memset(Tt, 0.0, w=["Tt"])
        P.memset(Sin, 0.0, w=["Sin"])
        for i in range(3):
            for h in range(8):
                P.stt(Tt[:, h, :], Tt[:, h, :], Aall[:, i, h:h + 1], Sall[:, i, h, :], ALU.mult, ALU.add,
                      r=["Tt", "Aall", "Sall"], w=["Tt"])
            P.stt(Sin, Tt, oneh[:, i + 1:i + 2], Sin, ALU.mult, ALU.add, r=["Tt", "oneh", "Sin"], w=["Sin"])
        P.copy(Sin_bf, Sin, r=["Sin"], w=["Sin_bf"], eng="dve")
    P.barrier()
    P.top = mark0
    xt = P.alloc([8, TT])
    uT = P.alloc([8, TT], BF16)
    xv = T["xT"].rearrange("(k p) t -> p k t", p=128)
    x1v = T["x1T"].rearrange("(k p) t -> p k t", p=128)
    xkeys = ["xt%d" % k for k in range(8)]
    if hgrn:
        ol = P.alloc([8, TT], BF16)
        qs = P.alloc([8, TT], BF16)
        sg = P.alloc([8, TT], BF16)
        o32 = [P.alloc([TT]), P.alloc([TT])]
        sq = [P.alloc([TT], BF16), P.alloc([TT], BF16)]
        rs = [P.alloc([TT]), P.alloc([TT])]
        olv = T["oloc"].rearrange("(k p) t -> p k t", p=128)
        qsv = T["qseg"].rearrange("(k p) t -> p k t", p=128)
        sgv = T["sgT"].rearrange("(k p) t -> p k t", p=128)
    else:
        uv = T["uT"].rearrange("(k p) t -> p k t", p=128)
    for tt in range(NTT):
        c0 = tt * TT
        P.load("sp", xt, xv[:, :, c0:c0 + TT], w=xkeys)
        if hgrn:
            P.load("sp", ol, olv[:, :, c0:c0 + TT], w=["ol"])
            P.load("act", qs, qsv[:, :, c0:c0 + TT], w=["qs"])
            P.load("act", sg, sgv[:, :, c0:c0 + TT], w=["sg"])
            for h in range(8):
                p = h % 2
                P.mm(P.bank(p), Sin_bf[:, h, :], qs[:, h, :], True, True, r=["Sin_bf", "qs"], w=[("b", p)])
                P.tt(o32[p], P.bank(p), ol[:, h, :], ALU.add, r=[("b", p), "ol"], w=[("o32", p)])
                P.act(sq[p], o32[p], AF.Square, r=[("o32", p)], w=[("sq", p)])
                P.mm(P.bank(2 + p), C.ones_128, sq[p], True, True, r=["c_ones_128", ("sq", p)], w=[("b", 2 + p)])
                P.act(rs[p], P.bank(2 + p), AF.Sqrt, r=[("b", 2 + p), "c_eps"], w=[("rs", p)],
                      bias=C.eps[:, 0:1], scale=1.0)
                P.op("dve", lambda e, p=p: e.reciprocal(rs[p], rs[p]), r=[("rs", p)], w=[("rs", p)])
                P.stt(o32[p], o32[p], onorm[:, 0:1], rs[p], ALU.mult, ALU.mult,
                      r=[("o32", p), "onorm", ("rs", p)], w=[("o32", p)])
                P.tt(uT[:, h, :], o32[p], sg[:, h, :], ALU.mult, r=[("o32", p), "sg"], w=[("uT", h)], eng="pool")
        else:
            P.load("act", uT, uv[:, :, c0:c0 + TT], w=[("uT", h) for h in range(8)])
        for n in range(8):
            b = 4 + n % 2
            for h in range(8):
                P.mm(P.bank(b), Wo[:, h, n * 128:(n + 1) * 128], uT[:, h, :], h == 0, h == 7,
                     r=["Wo", ("uT", h)], w=[("b", b)])
            P.stt(xt[:, n, :], P.bank(b), gate[:, n:n + 1], xt[:, n, :], ALU.mult, ALU.add,
                  r=[("b", b), "modsT", xkeys[n]], w=[xkeys[n]])
        P.store("sp", x1v[:, :, c0:c0 + TT], xt, r=xkeys, w=[("x1", tt)], final=False)


def ph_ffn(P, C, T, modsT, layer, extra=()):
    W1 = P.alloc([8, 2 * FH], BF16)
    W2 = P.alloc([FJ, D], BF16)
    gain = P.alloc([8])
    gs = P.alloc([8])
    egain = [P.alloc([8]) for _ in extra]
    egs = [P.alloc([8]) for _ in extra]
    mark0 = P.top
    stage = [P.alloc([1408]), P.alloc([1408])]
    load_cast_weight(P, T["w1"], W1, 8, 2 * FH, stage, "W1", 1408)
    load_cast_weight(P, T["w2"], W2, FJ, D, stage, "W2", 1024)
    off = (2 * layer + 1) * 24
    shift = modsT[:, off:off + 8]
    gate = modsT[:, off + 16:off + 24]
    P.load("act", gain, T["gainT"], w=["gain"])
    make_gs(P, gs, "gs", gain, "gain", modsT[:, off + 8:off + 16], "modsT")
    for i, (gd, sh, sc, od) in enumerate(extra):
        P.load("act", egain[i], gd, w=[("egain", i)])
        make_gs(P, egs[i], ("egs", i), egain[i], ("egain", i), sc, "modsT")
    P.barrier()
    P.top = mark0
    xt = P.alloc([8, TT])
    hT = P.alloc([8, TT], BF16)
    gT = P.alloc([FJ, TT], BF16)
    rstd = P.alloc([TT])
    tmp = [P.alloc([TT]), P.alloc([TT])]
    sa = [P.alloc([TT]), P.alloc([TT])]
    xv = T["x1T"].rearrange("(k p) t -> p k t", p=128)
    yv = T["x2T"].rearrange("(k p) t -> p k t", p=128)
    xkeys = ["xt%d" % k for k in range(8)]
    hkeys = [("hT", k) for k in range(8)]
    for tt in range(NTT):
        c0 = tt * TT
        P.load("sp", xt, xv[:, :, c0:c0 + TT], r=[("x1", tt)], w=xkeys)
        modulate(P, C, xt, xkeys, hT, hkeys, gs, "gs", shift, "modsT", P.bank(0), ("b", 0), rstd, tmp)
        for j in range(FJ):
            ba, bb = 1 + j % 2, 3 + j % 2
            for k in range(KT):
                P.mm(P.bank(ba), W1[:, k, j * 128:(j + 1) * 128], hT[:, k, :], k == 0, k == KT - 1,
                     r=["W1", hkeys[k]], w=[("b", ba)])
            for k in range(KT):
                P.mm(P.bank(bb), W1[:, k, FH + j * 128:FH + (j + 1) * 128], hT[:, k, :], k == 0, k == KT - 1,
                     r=["W1", hkeys[k]], w=[("b", bb)])
            P.act(sa[j % 2], P.bank(ba), AF.Silu, r=[("b", ba)], w=[("sa", j % 2)])
            P.tt(gT[:, j, :], sa[j % 2], P.bank(bb), ALU.mult, r=[("sa", j % 2), ("b", bb)], w=[("gT", j)])
        for n in range(KT):
            bo = 5 + n % 2
            for j in range(FJ):
                P.mm(P.bank(bo), W2[:, j, n * 128:(n + 1) * 128], gT[:, j, :], j == 0, j == FJ - 1,
                     r=["W2", ("gT", j)], w=[("b", bo)])
            P.stt(xt[:, n, :], P.bank(bo), gate[:, n:n + 1], xt[:, n, :], ALU.mult, ALU.add,
                  r=[("b", bo), "modsT", xkeys[n]], w=[xkeys[n]])
        P.store("sp", yv[:, :, c0:c0 + TT], xt, r=xkeys)
        for i, (gd, sh, sc, od) in enumerate(extra):
            modulate(P, C, xt, xkeys, hT, hkeys, egs[i], ("egs", i), sh, "modsT", P.bank(0), ("b", 0), rstd, tmp)
            ov = od.rearrange("(k p) t -> p k t", p=128)
            P.store("sp", ov[:, :, c0:c0 + TT], hT, r=hkeys)


def host_consts():
    idx = np.arange(128)
    same = (idx[:, None] // 32) == (idx[None, :] // 32)
    IU = np.zeros((128, 256), np.float32)
    IU[:, :128] = np.eye(128)
    IU[:, 128:] = (same & (idx[:, None] <= idx[None, :])).astype(np.float32)
    Lrev = (same & (idx[:, None] > idx[None, :])).astype(np.float32)
    maskneg = (-(same & (idx[None, :] >= idx[:, None])).astype(np.float32)).astype(NPBF)
    rowmask = (idx[:, None] // 32 == np.arange(4)[None, :]).astype(np.float32)
    return dict(cIU=IU, cLrev=np.ascontiguousarray(Lrev), cmaskneg=np.ascontiguousarray(maskneg),
                crowmask=np.ascontiguousarray(rowmask))


def _hrec_io(P, T, lyr):
    T["w_in"] = P.din("w_in", [D, 4096])
    T["gainT"] = P.din("gain_mix", [128, 8])
    T["cIU"] = P.din("cIU", [128, 256])
    T["cLrev"] = P.din("cLrev", [128, 128])
    T["cmaskneg"] = P.din("cmaskneg", [128, 128], BF16)
    T["crowmask"] = P.din("crowmask", [128, 4])
    if lyr > 0:
        T["lbraw"] = P.din("lbraw", [128, 2, 1024])
    T["oloc"] = P.dout("oloc_o", [D, TC], BF16)
    T["qseg"] = P.dout("qseg_o", [D, TC], BF16)
    T["sgT"] = P.dout("sgT_o", [D, TC], BF16)
    T["Sloc"] = P.dout("Sloc_o", [128, 8, 128])
    T["Aseg"] = P.dout("Aseg_o", [128, 8])


def _oproj_hgrn_io(P, T):
    T["oloc"] = P.din("oloc", [D, TC], BF16)
    T["qseg"] = P.din("qseg", [D, TC], BF16)
    T["sgT"] = P.din("sgT", [D, TC], BF16)
    T["Sall"] = P.din("Sall", [128, 4, 8, 128])
    T["Aall"] = P.din("Aall", [128, 4, 8])
    T["onehot"] = P.din("onehot", [128, 4])
    T["onormT"] = P.din("onormT", [128, 1])
    T["w_out"] = P.din("w_out", [D, D])


def _ffn_io(P, T):
    T["w1"] = P.din("w1", [D, 2 * FH])
    T["w2"] = P.din("w2", [FH, D])
    T["gainT"] = P.din("gain_ffn", [128, 8])


def _start(P, load_mods=True):
    modsT = P.alloc([NMOD])
    C = setup_consts(P, None)
    if load_mods:
        mi = P.din("modsT_in", [128, NMOD])
        P.load("sp", modsT, mi, w=["modsT"])
    P.set_mark()
    return modsT, C


def build_L0():
    P = Prog()
    P.set_mark()
    T = dict(cT=P.din("cT", [128, 8]), ada_w=P.din("ada_w", [8, D, 768]), mod_bT=P.din("mod_bT", [128, 52]),
             kv_ada_w=P.din("kv_ada_w", [D, 512]))
    mo = P.dout("mods_o", [128, 52])
    msb = P.alloc([52])
    ph_mods(P, T, msb)
    P.store("sp", mo, msb, r=["mods_out"])
    return P.build()


def build_L1():
    P = Prog()
    modsT, C = _start(P)
    T2 = dict(xT=P.din("xT", [D, TC]))
    _hrec_io(P, T2, 0)
    ph_hrec(P, C, T2, modsT, 0)
    return P.build()


def build_L2():
    P = Prog()
    modsT, C = _start(P)
    T = dict(xT=P.din("xT", [D, TC]), x1T=P.dscratch("x1T", [D, TC]))
    _oproj_hgrn_io(P, T)
    ph_oproj(P, C, T, modsT, 0, True)
    P.new_phase()
    T2 = dict(x1T=T["x1T"], x2T=P.dout("x2T", [D, TC]))
    _ffn_io(P, T2)
    ph_ffn(P, C, T2, modsT, 0)
    P.new_phase()
    T3 = dict(xT=T2["x2T"])
    _hrec_io(P, T3, 1)
    ph_hrec(P, C, T3, modsT, 1)
    return P.build()


def build_L3():
    P = Prog()
    modsT, C = _start(P)
    T = dict(xT=P.din("xT", [D, TC]), x1T=P.dscratch("x1T", [D, TC]))
    _oproj_hgrn_io(P, T)
    ph_oproj(P, C, T, modsT, 1, True)
    P.new_phase()
    T2 = dict(x1T=T["x1T"], x2T=P.dout("x2T", [D, TC]))
    _ffn_io(P, T2)
    extra = [(P.din("gain_kv", [128, 8]), modsT[:, 192:200], modsT[:, 200:208], P.dout("hkvT", [D, TC], BF16)),
             (P.din("gain_mixn", [128, 8]), modsT[:, 96:104], modsT[:, 104:112], P.dout("hmixT", [D, TC], BF16))]
    ph_ffn(P, C, T2, modsT, 1, extra)
    return P.build()


_PROGS = {}


def get_prog(name, builder):
    if name not in _PROGS:
        _PROGS[name] = builder()
    return _PROGS[name]


class HostState:
    def __init__(self, inp):
        self.inp = {k: np.asarray(v) for k, v in inp.items()}
        self.hc = host_consts()
        x = self.inp["x"]
        self.xT = [np.ascontiguousarray(x[c // 4, (c % 4) * TC:(c % 4 + 1) * TC, :].T) for c in range(8)]

    def run_L0(self):
        inp = self.inp
        nc = get_prog("L0", build_L0)
        maps = []
        for c in range(8):
            b, r = c // 4, c % 4
            mod_b = np.zeros((128, 52), np.float32)
            for s in range(8):
                mod_b[:, s * 6:(s + 1) * 6] = inp["ada_b"][s, 768 * r:768 * (r + 1)].reshape(6, 128).T
            mod_b[:, 48:52] = inp["kv_ada_b"][512 * r:512 * (r + 1)].reshape(4, 128).T
            maps.append(dict(cT=fm(inp["c"][b]),
                             ada_w=np.ascontiguousarray(inp["ada_w"][:, :, 768 * r:768 * (r + 1)]),
                             mod_bT=mod_b,
                             kv_ada_w=np.ascontiguousarray(inp["kv_ada_w"][:, 512 * r:512 * (r + 1)])))
        res = run_prog(nc, maps)
        self.modsT = []
        for b in range(2):
            m = np.zeros((128, NMOD), np.float32)
            for r in range(4):
                o = np.asarray(res[4 * b + r]["mods_o"])
                for s in range(8):
                    m[:, s * 24 + 6 * r:s * 24 + 6 * r + 6] = o[:, s * 6:(s + 1) * 6]
                m[:, 192 + 4 * r:192 + 4 * r + 4] = o[:, 48:52]
            self.modsT.append(m)

    def _hrec_inputs(self, layer):
        inp = self.inp
        d = dict(w_in=inp["hgrn_w_in"][layer], gain_mix=fm(inp["norm_mix"][layer]), **self.hc)
        if layer > 0:
            d["lbraw"] = np.ascontiguousarray(np.broadcast_to(inp["hgrn_lower_bounds"][None], (128, 2, 1024)))
        return d

    def _take_hrec(self, res):
        self.hrec = [{k: np.asarray(res[c][k + "_o"]) for k in ("oloc", "qseg", "sgT", "Sloc", "Aseg")}
                     for c in range(8)]

    def _oproj_hgrn_inputs(self, c, layer):
        inp = self.inp
        b, r = c // 4, c % 4
        h = self.hrec[c]
        oneh = np.zeros((128, 4), np.float32)
        oneh[:, r] = 1.0
        return dict(oloc=h["oloc"], qseg=h["qseg"], sgT=h["sgT"],
                    Sall=np.ascontiguousarray(np.stack([self.hrec[4 * b + i]["Sloc"] for i in range(4)], axis=1)),
                    Aall=np.ascontiguousarray(np.stack([self.hrec[4 * b + i]["Aseg"] for i in range(4)], axis=1)),
                    onehot=oneh, onormT=np.ascontiguousarray(inp["hgrn_out_norm"][layer].reshape(128, 1)),
                    w_out=inp["hgrn_w_out"][layer])

    def _ffn_inputs(self, layer):
        inp = self.inp
        return dict(w1=inp["ffn_w_in"][layer], w2=inp["ffn_w_out"][layer], gain_ffn=fm(inp["norm_ffn"][layer]))

    def run_L1(self):
        nc = get_prog("L1", build_L1)
        maps = [dict(modsT_in=self.modsT[c // 4], xT=self.xT[c], **self._hrec_inputs(0)) for c in range(8)]
        self._take_hrec(run_prog(nc, maps))

    def run_L2(self):
        nc = get_prog("L2", build_L2)
        maps = [dict(modsT_in=self.modsT[c // 4], xT=self.xT[c], **self._oproj_hgrn_inputs(c, 0),
                     **self._ffn_inputs(0), **self._hrec_inputs(1)) for c in range(8)]
        res = run_prog(nc, maps)
        self.xT = [np.asarray(res[c]["x2T"]) for c in range(8)]
        self._take_hrec(res)

    def run_L3(self):
        inp = self.inp
        nc = get_prog("L3", build_L3)
        maps = [dict(modsT_in=self.modsT[c // 4], xT=self.xT[c], **self._oproj_hgrn_inputs(c, 1),
                     **self._ffn_inputs(1), gain_kv=fm(inp["kv_norm"]), gain_mixn=fm(inp["norm_mix"][2]))
                for c in range(8)]
        res = run_prog(nc, maps)
        self.xT = [np.asarray(res[c]["x2T"]) for c in range(8)]
        self.hkvT = [np.asarray(res[c]["hkvT"]) for c in range(8)]
        self.hmixT = [np.asarray(res[c]["hmixT"]) for c in range(8)]


NQT = S_LEN // 128
NEGB = -30000.0


def nsa_host_consts():
    idx = np.arange(128)
    c = {}
    c["ident"] = np.eye(128, dtype=np.float32).astype(NPBF)
    blk = ((idx[:, None] // 64) == (idx[None, :] // 64)).astype(np.float32) / 64.0
    c["blk64"] = blk.astype(NPBF)
    RT = np.zeros((128, 128), np.float32)
    for base in (0, 64):
        for m in range(32):
            RT[base + m + 32, base + m] = -1.0
            RT[base + m + 32 - 32 + 0, base + m + 32] = 1.0
    c["RT2"] = RT.astype(NPBF)
    inv_freq = 1.0 / (10000.0 ** (np.arange(0, 64, 2, dtype=np.float32) / 64))
    ang = np.arange(S_LEN, dtype=np.float32)[None, :] * inv_freq[:, None]
    cos = np.cos(ang).astype(np.float32)
    sin = np.sin(ang).astype(np.float32)
    c["cosT"] = np.ascontiguousarray(np.tile(cos, (4, 1)))
    c["sinT"] = np.ascontiguousarray(np.tile(sin, (4, 1)))
    c["Dm"] = (16.0 * idx[:, None] + 31.0 - idx[None, :]).astype(np.float32)
    cs = np.arange(1024) * 16
    ss = np.arange(256) * 64
    ov = ((cs[:, None] < ss[None, :] + 64) & (cs[:, None] + 32 > ss[None, :])).astype(np.float32)
    ov[1023, :] = 0.0
    c["ov"] = np.ascontiguousarray(ov.reshape(8, 128, 256).transpose(1, 0, 2))
    c["tri01"] = (idx[:, None] <= idx[None, :]).astype(np.float32).astype(NPBF)
    c["low01"] = (idx[:, None] > idx[None, :]).astype(np.float32).astype(NPBF)
    keys = np.arange(S_LEN)
    Z = (((keys[None, :] // 64) % 64) == np.arange(64)[:, None]).astype(np.float32)
    c["Zpat"] = Z.astype(NPBF)
    SelG = np.zeros((12, 12, 64), np.float32)
    for i in range(12):
        SelG[i, i, :] = 1.0
    c["SelG"] = SelG
    Sh = np.zeros((128, 64), np.float32)
    Sh[64, :] = 1.0
    c["ShiftB"] = Sh
    return c


def head_norm_rope(P, C, N, ps, pkey, gvec, gkey, cosd, sind, c0, ncol, bank_n, bank_r, W, tag):
    P.act(W["sq"][:, 0:ncol], ps[:, 0:ncol], AF.Square, r=[pkey], w=["n_sq"])
    P.mm(P.bank(bank_n)[:, 0:ncol], N["blk64"], W["sq"][:, 0:ncol], True, True, r=["n_sq", "blk64"], w=[("b", bank_n)])
    P.act(W["rs"][:, 0:ncol], P.bank(bank_n)[:, 0:ncol], AF.Sqrt, r=[("b", bank_n), "c_eps"], w=["n_rs"],
          bias=C.eps[:, 0:1], scale=1.0)
    P.op("dve", lambda e: e.reciprocal(W["rs"][:, 0:ncol], W["rs"][:, 0:ncol]), r=["n_rs"], w=["n_rs"])
    P.stt(W["kn"][:, 0:ncol], ps[:, 0:ncol], gvec, W["rs"][:, 0:ncol], ALU.mult, ALU.mult,
          r=[pkey, gkey, "n_rs"], w=["n_kn"])
    P.copy(W["knb"][:, 0:ncol], W["kn"][:, 0:ncol], r=["n_kn"], w=["n_knb"], eng="act")
    P.mm(P.bank(bank_r)[:, 0:ncol], N["RT2"], W["knb"][:, 0:ncol], True, True, r=["n_knb", "RT2"], w=[("b", bank_r)])
    P.load("sp", W["cos"][:, 0:ncol], cosd[:, c0:c0 + ncol], w=["n_cos"])
    P.load("sp", W["sin"][:, 0:ncol], sind[:, c0:c0 + ncol], w=["n_sin"])
    P.tt(W["t1"][:, 0:ncol], W["kn"][:, 0:ncol], W["cos"][:, 0:ncol], ALU.mult, r=["n_kn", "n_cos"], w=["n_t1"])
    P.tt(W["t2"][:, 0:ncol], P.bank(bank_r)[:, 0:ncol], W["sin"][:, 0:ncol], ALU.mult, r=[("b", bank_r), "n_sin"], w=["n_t2"])


def alloc_norm_work(P):
    return dict(sq=P.alloc([TT], BF16), rs=P.alloc([TT]), kn=P.alloc([TT]), knb=P.alloc([TT], BF16),
                cos=P.alloc([TT]), sin=P.alloc([TT]), t1=P.alloc([TT]), t2=P.alloc([TT]))


def nsa_load_consts(P, T):
    N = {}
    for nm, shp, dt in (("ident", [128], BF16), ("blk64", [128], BF16), ("RT2", [128], BF16), ("Dm", [128], F32),
                        ("ov", [8, 256], F32), ("tri01", [128], BF16), ("low01", [128], BF16),
                        ("ShiftB", [64], F32)):
        N[nm] = P.alloc(shp, dt)
        P.load("act", N[nm], T[nm], w=[nm])
    N["SelG"] = P.alloc([12, 64])
    P.load("act", N["SelG"][0:12, :, :], T["SelG"], w=["SelG"])
    N["ones_c"] = P.alloc([128], BF16)
    P.memset(N["ones_c"], 1.0, w=["ones_c"])
    return N


def nsa_const_io(P, T):
    T["ident"] = P.din("ident", [128, 128], BF16)
    T["blk64"] = P.din("blk64", [128, 128], BF16)
    T["RT2"] = P.din("RT2", [128, 128], BF16)
    T["Dm"] = P.din("Dm", [128, 128])
    T["ov"] = P.din("ov", [128, 8, 256])
    T["tri01"] = P.din("tri01", [128, 128], BF16)
    T["low01"] = P.din("low01", [128, 128], BF16)
    T["ShiftB"] = P.din("ShiftB", [128, 64])
    T["SelG"] = P.din("SelG", [12, 12, 64])
    T["cosT"] = P.din("cosT", [128, S_LEN])
    T["sinT"] = P.din("sinT", [128, S_LEN])
    T["Zpat"] = P.din("Zpat", [64, S_LEN], BF16)


def alloc_kv(P):
    KV = {}
    KV["KselE"] = P.alloc([S_LEN], BF16)
    KV["KwinT"] = P.alloc([S_LEN], BF16)
    KV["Vsel"] = P.alloc([128, 66], BF16)
    KV["Vwin"] = P.alloc([128, 66], BF16)
    KV["Vc"] = P.alloc([8, 64], BF16)
    return KV


def ph_kvp(P, C, N, T, KV):
    Wkv = P.alloc([8, 384], BF16)
    knorm = P.alloc([2])
    KVraw = P.alloc([S_LEN + 16], BF16)
    mark1 = P.top
    stage = [P.alloc([2048]), P.alloc([2048])]
    load_cast_weight(P, T["wkv"], Wkv, 8, 384, stage, "Wkv", 384)
    P.load("act", knorm, T["knorm"], w=["knorm"])
    P.load("act", KV["KselE"][0:64, :], T["Zpat"], w=["KselE_z"])
    P.memset(KVraw[:, S_LEN:S_LEN + 16], 0.0, w=["KVraw_tail"])
    P.memset(KV["Vsel"][:, :, 64:66], 1.0, w=["Vsel_ones"])
    P.memset(KV["Vwin"][:, :, 64:66], 1.0, w=["Vwin_ones"])
    P.barrier()
    P.top = mark1
    hk = [P.alloc([8, TT], BF16), P.alloc([8, TT], BF16)]
    W = alloc_norm_work(P)
    hv = T["hkvT_all"].rearrange("(k p) t -> p k t", p=128)
    for tt in range(DBG.get("kvp_ntt", S_LEN // TT)):
        c0 = tt * TT
        h_ = hk[tt % 2]
        hkey = ("hk", tt % 2)
        P.load("sp", h_, hv[:, :, c0:c0 + TT], w=[hkey])
        part = DBG.get("kvp_part", 15)
        if part & 1:
            for k in range(8):
                P.mm(P.bank(0), Wkv[:, k, 0:128], h_[:, k, :], k == 0, k == 7, r=["Wkv", hkey], w=[("b", 0)])
            head_norm_rope(P, C, N, P.bank(0), ("b", 0), knorm[:, 0:1], "knorm", T["cosT"], T["sinT"], c0, TT, 1, 2, W, "k")
        if part & 8:
            P.tt(KV["KwinT"][0:64, c0:c0 + TT], W["t1"][0:64, :], W["t2"][0:64, :], ALU.add, r=["n_t1", "n_t2"], w=["Kwin"])
            P.tt(KV["KselE"][64:128, c0:c0 + TT], W["t1"][64:128, :], W["t2"][64:128, :], ALU.add,
                 r=["n_t1", "n_t2"], w=["Ksel"])
        if part & 2:
            for k in range(8):
                P.mm(P.bank(3), Wkv[:, k, 128:256], h_[:, k, :], k == 0, k == 7, r=["Wkv", hkey], w=[("b", 3)])
            P.copy(KVraw[:, c0:c0 + TT], P.bank(3), r=[("b", 3)], w=["KVraw"], eng="act")
        if DBG.get("kvp_bar", False):
            P.barrier()
    P.barrier()
    for tt in range(0 if DBG.get("kvp_skip_v") else DBG.get("kvp_ntt", S_LEN // TT)):
        c0 = tt * TT
        h_ = hk[tt % 2]
        hkey = ("hk", tt % 2)
        P.load("sp", h_, hv[:, :, c0:c0 + TT], w=[hkey])
        for sub in range(4):
            b = DBG.get("vbank", 4) + sub % 2
            kt = tt * 4 + sub
            for k in range(8):
                P.mm(P.bank(b)[:, 0:128], h_[:, k, sub * 128:(sub + 1) * 128], Wkv[:, k, 256:384], k == 0, k == 7,
                     r=["Wkv", hkey], w=[("b", b)])
            if DBG.get("v_scratch"):
                P.copy(W["knb"][:, 0:64], P.bank(b)[:, 0:64], r=[("b", b)], w=["Vsel"], eng="act")
                P.copy(W["knb"][:, 64:128], P.bank(b)[:, 64:128], r=[("b", b)], w=["Vwin"], eng="dve")
            elif DBG.get("v_dveonly", True):
                P.copy(KV["Vsel"][:, kt, 0:64], P.bank(b)[:, 0:64], r=[("b", b)], w=["Vsel"], eng="dve")
                P.copy(KV["Vwin"][:, kt, 0:64], P.bank(b)[:, 64:128], r=[("b", b)], w=["Vwin"], eng="dve")
            else:
                P.copy(KV["Vsel"][:, kt, 0:64], P.bank(b)[:, 0:64], r=[("b", b)], w=["Vsel"], eng="act")
                P.copy(KV["Vwin"][:, kt, 0:64], P.bank(b)[:, 64:128], r=[("b", b)], w=["Vwin"], eng="dve")
    P.barrier()
    P.top = mark1
    if DBG.get("kvp_stage", 9) < 1:
        return
    W1c = P.alloc([32, 256], BF16)
    W2k = P.alloc([2, 128], BF16)
    W2v = P.alloc([2, 64], BF16)
    posT = P.alloc([32, 2], BF16)
    posw = P.alloc([4])
    hid = [P.alloc([2, 1024], BF16), P.alloc([2, 1024], BF16)]
    sq = P.alloc([TT], BF16)
    rs = P.alloc([TT])
    stage = [P.alloc([2048]), P.alloc([2048])]
    for i in range(4):
        sg = stage[i % 2]
        skey = ("stage", i % 2)
        P.load("sp", sg, T["w1c"][:, 8 * i:8 * i + 8, :].rearrange("p a b -> p (a b)"), w=[skey])
        P.copy(W1c[:, 8 * i:8 * i + 8, :].rearrange("p a b -> p (a b)"), sg, r=[skey], w=["W1c"], eng="pool")
    s2 = P.alloc([256 + 128 + 64])
    P.load("act", s2[:, 0:256], T["w2k"].rearrange("p a b -> p (a b)"), w=["s2"])
    P.load("act", s2[:, 256:384], T["w2v"].rearrange("p a b -> p (a b)"), w=["s2"])
    P.load("act", s2[:, 384:448], T["posT2"].rearrange("p a b -> p (a b)"), w=["s2"])
    P.copy(W2k.rearrange("p a b -> p (a b)"), s2[:, 0:256], r=["s2"], w=["W2k"], eng="pool")
    P.copy(W2v.rearrange("p a b -> p (a b)"), s2[:, 256:384], r=["s2"], w=["W2v"], eng="pool")
    P.copy(posT.rearrange("p a b -> p (a b)"), s2[:, 384:448], r=["s2"], w=["posT"], eng="pool")
    rawv = KVraw.rearrange("p (c s) -> p s c", s=16)
    for kv in range(2):
        pr = slice(0, 64) if kv == 0 else slice(64, 128)
        for ht in range(2):
            pb = P.bank(kv)
            for l in range(32):
                P.mm(pb[:, 0:2], W1c[pr, l, ht * 128:(ht + 1) * 128], posT[pr, l, :], l == 0, l == 31,
                     r=["W1c", "posT"], w=[("b", kv)])
            P.copy(posw[:, kv * 2 + ht:kv * 2 + ht + 1], pb[:, 0:1], r=[("b", kv)], w=["posw"], eng="dve")
            for cch in range(2):
                pb2 = P.bank(2 + 2 * kv + cch)
                pk2 = ("b", 2 + 2 * kv + cch)
                for l in range(32):
                    if l < 16:
                        rhs = rawv[pr, l, cch * 512:cch * 512 + 512]
                    else:
                        rhs = rawv[pr, l - 16, cch * 512 + 1:cch * 512 + 513]
                    P.mm(pb2, W1c[pr, l, ht * 128:(ht + 1) * 128], rhs, l == 0, l == 31, r=["W1c"], w=[pk2])
                P.act(hid[kv][:, ht, cch * 512:(cch + 1) * 512], pb2, AF.Silu, r=[pk2, "posw"], w=[("hid", kv)],
                      bias=posw[:, kv * 2 + ht:kv * 2 + ht + 1], scale=1.0)
    if DBG.get("kvp_stage", 9) < 2:
        return
    for cch in range(2):
        for ht in range(2):
            P.mm(P.bank(6), W2k[:, ht, :], hid[0][:, ht, cch * 512:(cch + 1) * 512], ht == 0, ht == 1,
                 r=["W2k", ("hid", 0)], w=[("b", 6)])
        P.act(sq, P.bank(6), AF.Square, r=[("b", 6)], w=["n_sq"])
        P.mm(P.bank(7), N["blk64"], sq, True, True, r=["n_sq"], w=[("b", 7)])
        P.act(rs, P.bank(7), AF.Sqrt, r=[("b", 7)], w=["n_rs"], bias=C.eps[:, 0:1], scale=1.0)
        P.op("dve", lambda e: e.reciprocal(rs, rs), r=["n_rs"], w=["n_rs"])
        P.stt(KV["KwinT"][64:128, cch * 512:(cch + 1) * 512], P.bank(6)[64:128, :], knorm[64:128, 1:2],
              rs[64:128, :], ALU.mult, ALU.mult, r=[("b", 6), "n_rs"], w=["Kc"])
    for ct in range(8):
        b = ct % 2
        for ht in range(2):
            P.mm(P.bank(b)[:, 0:64], hid[1][:, ht, ct * 128:(ct + 1) * 128], W2v[:, ht, :], ht == 0, ht == 1,
                 r=[("hid", 1), "W2v"], w=[("b", b)])
        P.copy(KV["Vc"][:, ct, :], P.bank(b)[:, 0:64], r=[("b", b)], w=["Vc"], eng="act")
    if DBG.get("kvp_stage", 9) < 3:
        return
    P.store("sp", T["o_Ksel"], KV["KselE"][64:128, :], r=["Ksel"])
    P.store("sp", T["o_Kwin"], KV["KwinT"][0:64, :], r=["Kwin"])
    P.store("sp", T["o_Kc"], KV["KwinT"][64:128, 0:1024], r=["Kc"])
    P.store("sp", T["o_Vsel"], KV["Vsel"], r=["Vsel"])
    P.store("sp", T["o_Vwin"], KV["Vwin"], r=["Vwin"])
    P.store("sp", T["o_Vc"], KV["Vc"], r=["Vc"])


def ph_kvload(P, T, KV):
    P.load("sp", KV["KselE"][64:128, :], T["i_Ksel"], w=["Ksel"])
    P.load("sp", KV["KwinT"][0:64, :], T["i_Kwin"], w=["Kwin"])
    P.load("sp", KV["KwinT"][64:128, 0:1024], T["i_Kc"], w=["Kc"])
    P.load("act", KV["Vsel"], T["i_Vsel"], w=["Vsel"])
    P.load("act", KV["Vwin"], T["i_Vwin"], w=["Vwin"])
    P.load("act", KV["Vc"], T["i_Vc"], w=["Vc"])
    P.load("act", KV["KselE"][0:64, :], T["Zpat"], w=["KselE_z"])


def ph_qp(P, C, N, T):
    Wq = P.alloc([8, 256], BF16)
    Wg = P.alloc([8, 12], BF16)
    qnorm = P.alloc([1])
    mark0 = P.top
    stage = [P.alloc([2048]), P.alloc([2048])]
    load_cast_weight(P, T["wq"], Wq, 8, 256, stage, "Wq", 256)
    load_cast_weight(P, T["wg"], Wg, 8, 12, stage, "Wg", 12)
    P.load("act", qnorm, T["qnorm"], w=["qnorm"])
    P.ts(qnorm, qnorm, 0.125, None, ALU.mult, None, r=["qnorm"], w=["qnorm"])
    P.barrier()
    P.top = mark0
    hm = [P.alloc([8, TT], BF16), P.alloc([8, TT], BF16)]
    W = alloc_norm_work(P)
    qn_b = P.alloc([TT], BF16)
    qr_b = P.alloc([TT], BF16)
    g12 = P.alloc([TT])
    hv = T["hmixT_all"].rearrange("(k p) t -> p k t", p=128)
    for tt in range(S_LEN // TT):
        c0 = tt * TT
        h_ = hm[tt % 2]
        hkey = ("hm", tt % 2)
        P.load("sp", h_, hv[:, :, c0:c0 + TT], w=[hkey])
        for pp in range(2):
            for k in range(8):
                P.mm(P.bank(0), Wq[:, k, pp * 128:(pp + 1) * 128], h_[:, k, :], k == 0, k == 7,
                     r=["Wq", hkey], w=[("b", 0)])
            head_norm_rope(P, C, N, P.bank(0), ("b", 0), qnorm[:, 0:1], "qnorm", T["cosT"], T["sinT"], c0, TT, 1, 2, W, "q")
            P.copy(qn_b, W["kn"], r=["n_kn"], w=["qn_b"], eng="pool")
            P.tt(qr_b, W["t1"], W["t2"], ALU.add, r=["n_t1", "n_t2"], w=["qr_b"])
            for gg in range(2):
                g = 2 * pp + gg
                P.store("sp", T["QnD"][:, g, c0:c0 + TT], qn_b[gg * 64:(gg + 1) * 64, :], r=["qn_b"], final=False)
                P.store("sp", T["QrD"][:, g, c0:c0 + TT], qr_b[gg * 64:(gg + 1) * 64, :], r=["qr_b"], final=False)
        for k in range(8):
            P.mm(P.bank(3)[0:12, :], Wg[:, k, :], h_[:, k, :], k == 0, k == 7, r=["Wg", hkey], w=[("b", 3)])
        P.act(g12[0:12, :], P.bank(3)[0:12, :], AF.Sigmoid, r=[("b", 3)], w=["g12"])
        P.store("sp", T["G12D"][:, c0:c0 + TT], g12[0:12, :], r=["g12"], final=False)


def ph_att(P, C, N, T, KV):
    QB = [P.alloc([4, 128], BF16), P.alloc([4, 128], BF16)]
    QE = [[P.alloc([4, 128], BF16) for _ in range(4)] for _ in range(2)]
    G12t = [P.alloc([128]), P.alloc([128])]
    Pc = [P.alloc([512], BF16) for _ in range(8)]
    Pt = [P.alloc([512], BF16) for _ in range(3)]
    m01 = P.alloc([128], BF16)
    rinv = P.alloc([512])
    Pn = P.alloc([512])
    Pg = P.alloc([8, 128])
    imp = P.alloc([256])
    imp2 = P.alloc([256])
    m8a = P.alloc([8])
    m8b = P.alloc([8])
    thr = P.alloc([1])
    selb = P.alloc([256], BF16)
    Osb = [P.alloc([512]), P.alloc([512])]
    rden = P.alloc([512])
    gate = [P.alloc([512]) for _ in range(3)]
    acc = P.alloc([512])
    tq = P.alloc([512])
    ofin = P.alloc([4, 128], BF16)
    P.memset(Osb[0], 0.0, w=["Osb0"])
    P.memset(Osb[1], 0.0, w=["Osb1"])
    psT = P.bank(5, BF16)
    pti = 0

    def g3(ap2d):
        return ap2d.rearrange("p (g q) -> p g q", g=4)

    def bc(mask):
        return mask.unsqueeze(1).broadcast_to([128, 4, 128])

    for qt in DBG.get("att_list", range(DBG.get("att_nqt", NQT))):
        t0 = 128 * qt
        par = DBG.get("att_par", qt % 2)
        nW = (2 * qt + 1) // 64 + 1
        qb = QB[par]
        qbk = ("QB", par)
        qb2 = qb.rearrange("p g q -> p (g q)")
        P.load("sp", qb[0:64, :, :], T["QrD"][:, :, t0:t0 + 128], w=[qbk])
        P.load("sp", qb[64:128, :, :], T["QnD"][:, :, t0:t0 + 128], w=[qbk])
        for w in range(nW):
            P.load("sp", QE[par][w][64:128, :, :], T["QrD"][:, :, t0:t0 + 128], w=[("QEq", par, w)])
        P.load("sp", G12t[par][0:12, :], T["G12D"][:, t0:t0 + 128], w=[("G12", par)])
        n_ct = (8 * qt + 6) // 128 + 1
        for ct in range(n_ct):
            P.mm(P.bank(0), KV["KwinT"][64:128, ct * 128:(ct + 1) * 128], qb2[64:128, :], True, True,
                 r=[qbk], w=[("b", 0)])
            P.act(Pc[ct], P.bank(0), AF.Exp, r=[("b", 0)], w=[("Pc", ct)])
            th = t0 - 2048 * ct
            if th < 2063:
                P.ts(m01, N["Dm"], float(th), None, ALU.is_le, None, r=[], w=["m01"])
                P.tt(g3(Pc[ct]), g3(Pc[ct]), bc(m01), ALU.mult, r=[("Pc", ct), "m01"], w=[("Pc", ct)])
        for ct in range(n_ct):
            P.mm(P.bank(2), N["ones_c"], Pc[ct], ct == 0, ct == n_ct - 1, r=[("Pc", ct)], w=[("b", 2)])
        for ct in range(n_ct):
            P.mm(P.bank(3)[0:64, :], KV["Vc"][:, ct, :], Pc[ct], ct == 0, ct == n_ct - 1, r=[("Pc", ct)], w=[("b", 3)])
        P.ts(rinv, P.bank(2), 1e-30, None, ALU.max, None, r=[("b", 2)], w=["rinv"])
        P.op("dve", lambda e: e.reciprocal(rinv, rinv), r=["rinv"], w=["rinv"])
        for ct in range(n_ct):
            P.tt(Pn, Pc[ct], rinv, ALU.mult, r=[("Pc", ct), "rinv"], w=["Pn"])
            P.op("dve", lambda e, ct=ct: e.tensor_reduce(Pg[:, ct, :], Pn.rearrange("p (g q) -> p q g", g=4),
                                                         mybir.AxisListType.X, ALU.add),
                 r=["Pn"], w=[("Pg", ct)])
        for ct in range(n_ct):
            P.mm(P.bank(4)[:, 0:256], Pg[:, ct, :], N["ov"][:, ct, :], ct == 0, ct == n_ct - 1,
                 r=[("Pg", ct)], w=[("b", 4)])
        P.copy(imp, P.bank(4)[:, 0:256], r=[("b", 4)], w=["imp"], eng="dve")
        ncv = 2 * qt + 2
        if ncv < 256:
            P.memset(imp[:, ncv:256], -1.0, w=["imp"])
        P.memset(imp[:, 2 * qt:2 * qt + 1], 1e9, w=["imp"])
        P.memset(imp[64:128, 2 * qt + 1:2 * qt + 2], 1e9, w=["imp"])
        P.memset(imp[0:64, 2 * qt + 1:2 * qt + 2], -1.0, w=["imp"])
        if qt >= 1:
            P.memset(imp[0:64, 2 * qt - 1:2 * qt], 1e9, w=["imp"])
        P.memset(imp[:, 0:1], 1e9, w=["imp"])
        P.op("dve", lambda e: e.max(m8a, imp), r=["imp"], w=["m8a"])
        P.op("dve", lambda e: e.match_replace(imp2, m8a, imp, -2.0), r=["imp", "m8a"], w=["imp2"])
        P.op("dve", lambda e: e.max(m8b, imp2), r=["imp2"], w=["m8b"])
        P.ts(thr, m8b[:, 7:8], 0.0, None, ALU.max, None, r=["m8b"], w=["thr"])
        P.ts(selb, imp, thr[:, 0:1], NEGB, ALU.is_lt, ALU.mult, r=["imp", "thr"], w=["selb"])
        for w in range(nW):
            P.tr(psT[0:64, 0:128], selb[:, 64 * w:64 * w + 64], N["ident"], r=["selb"], w=[("b", 5)])
            P.op("dve", lambda e, w=w, par=par: e.tensor_copy(QE[par][w][0:64, :, :],
                                                     psT[0:64, 0:128].unsqueeze(1).broadcast_to([64, 4, 128])),
                 r=[("b", 5)], w=[("QEs", par, w)])
        for kt in range(qt + 1):
            w = kt // 32
            b = kt % 2
            P.mm(P.bank(b), KV["KselE"][:, kt * 128:(kt + 1) * 128], QE[par][w].rearrange("p g q -> p (g q)"),
                 True, True, r=[("QEq", par, w), ("QEs", par, w)], w=[("b", b)])
            pt, ptk = Pt[pti % 3], ("Pt", pti % 3)
            pti += 1
            P.act(pt, P.bank(b), AF.Exp, r=[("b", b)], w=[ptk])
            if kt == qt:
                P.tt(g3(pt), g3(pt), bc(N["tri01"]), ALU.mult, r=[ptk], w=[ptk])
            P.mm(P.bank(6)[0:66, :], KV["Vsel"][:, kt, :], pt, kt == 0, kt == qt, r=[ptk], w=[("b", 6)])
        k0 = max(0, qt - 4)
        for kt in range(k0, qt + 1):
            P.mm(P.bank(1), KV["KwinT"][0:64, kt * 128:(kt + 1) * 128], qb2[0:64, :], True, True,
                 r=[qbk], w=[("b", 1)])
            pt, ptk = Pt[pti % 3], ("Pt", pti % 3)
            pti += 1
            P.act(pt, P.bank(1), AF.Exp, r=[("b", 1)], w=[ptk])
            if kt == qt:
                P.tt(g3(pt), g3(pt), bc(N["tri01"]), ALU.mult, r=[ptk], w=[ptk])
            if kt == qt - 4:
                P.tt(g3(pt), g3(pt), bc(N["low01"]), ALU.mult, r=[ptk], w=[ptk])
            P.mm(P.bank(7)[0:66, :], KV["Vwin"][:, kt, :], pt, kt == k0, kt == qt, r=[ptk], w=[("b", 7)])
        for br in range(3):
            for g in range(4):
                P.mm(P.bank(5)[0:64, g * 128:(g + 1) * 128], N["SelG"][0:12, br * 4 + g, :], G12t[par][0:12, :],
                     True, True, r=[("G12", par)], w=[("b", 5)])
            P.copy(gate[br][0:64, :], P.bank(5)[0:64, :], r=[("b", 5)], w=[("gate", br)], eng="dve")
        P.tt(acc[0:64, :], P.bank(3)[0:64, :], rinv[0:64, :], ALU.mult, r=[("b", 3), "rinv"], w=["acc"])
        P.tt(acc[0:64, :], acc[0:64, :], gate[0][0:64, :], ALU.mult, r=["acc", ("gate", 0)], w=["acc"])
        for i, bk in ((0, 6), (1, 7)):
            ok = "Osb%d" % i
            P.copy(Osb[i][0:66, :], P.bank(bk)[0:66, :], r=[("b", bk)], w=[ok], eng="dve")
            P.mm(P.bank(2)[0:64, :], N["ShiftB"], Osb[i], True, True, r=[ok], w=[("b", 2)])
            P.op("dve", lambda e: e.reciprocal(rden[0:64, :], P.bank(2)[0:64, :]), r=[("b", 2)], w=["rden"])
            P.tt(tq[0:64, :], Osb[i][0:64, :], rden[0:64, :], ALU.mult, r=[ok, "rden"], w=["tq"])
            P.tt(tq[0:64, :], tq[0:64, :], gate[1 + i][0:64, :], ALU.mult, r=["tq", ("gate", 1 + i)], w=["tq"])
            P.tt(acc[0:64, :], acc[0:64, :], tq[0:64, :], ALU.add, r=["acc", "tq"], w=["acc"])
        P.copy(ofin[0:64, :, :].rearrange("p g q -> p (g q)"), acc[0:64, :], r=["acc"], w=["ofin"], eng="dve")
        P.store("sp", T["oTD"][:, :, t0:t0 + 128], ofin[0:64, :, :], r=["ofin"])


def build_nsa(with_kvp):
    P = Prog()
    C = setup_consts(P, None)
    T = {}
    nsa_const_io(P, T)
    N = nsa_load_consts(P, T)
    KV = alloc_kv(P)
    P.set_mark()
    if with_kvp:
        T.update(hkvT_all=P.din("hkvT_all", [D, S_LEN], BF16), wkv=P.din("wkv", [D, 384]),
                 knorm=P.din("knorm", [128, 2]), w1c=P.din("w1c", [128, 32, 256]), posT2=P.din("posT2", [128, 32, 2]),
                 w2k=P.din("w2k", [128, 2, 128]), w2v=P.din("w2v", [128, 2, 64]),
                 o_Ksel=P.dout("o_Ksel", [64, S_LEN], BF16), o_Kwin=P.dout("o_Kwin", [64, S_LEN], BF16),
                 o_Kc=P.dout("o_Kc", [64, 1024], BF16), o_Vsel=P.dout("o_Vsel", [128, 128, 66], BF16),
                 o_Vwin=P.dout("o_Vwin", [128, 128, 66], BF16), o_Vc=P.dout("o_Vc", [128, 8, 64], BF16))
        if not DBG.get("skip_kvp"):
            ph_kvp(P, C, N, T, KV)
    else:
        T.update(i_Ksel=P.din("i_Ksel", [64, S_LEN], BF16), i_Kwin=P.din("i_Kwin", [64, S_LEN], BF16),
                 i_Kc=P.din("i_Kc", [64, 1024], BF16), i_Vsel=P.din("i_Vsel", [128, 128, 66], BF16),
                 i_Vwin=P.din("i_Vwin", [128, 128, 66], BF16), i_Vc=P.din("i_Vc", [128, 8, 64], BF16))
        ph_kvload(P, T, KV)
    P.new_phase()
    T.update(hmixT_all=P.din("hmixT_all", [D, S_LEN], BF16), wq=P.din("wq", [D, 256]), wg=P.din("wg", [D, 12]),
             qnorm=P.din("qnorm", [128, 1]),
             QnD=P.dscratch("QnD", [64, 4, S_LEN], BF16), QrD=P.dscratch("QrD", [64, 4, S_LEN], BF16),
             G12D=P.dscratch("G12D", [12, S_LEN]), oTD=P.dout("oTD", [64, 4, S_LEN], BF16))
    if not DBG.get("skip_qp"):
        ph_qp(P, C, N, T)
    P.new_phase()
    if not DBG.get("skip_att"):
        ph_att(P, C, N, T, KV)
    return P.build()


def build_L5(last):
    P = Prog()
    modsT, C = _start(P)
    layer = 3 if last else 2
    T = dict(xT=P.din("xT", [D, TC]), x1T=P.dscratch("x1T", [D, TC]), uT=P.din("uT", [D, TC], BF16),
             w_out=P.din("w_out", [D, D]))
    ph_oproj(P, C, T, modsT, layer, False)
    P.new_phase()
    T2 = dict(x1T=T["x1T"], x2T=P.dout("x2T", [D, TC]))
    _ffn_io(P, T2)
    extra = []
    if not last:
        extra = [(P.din("gain_mixn", [128, 8]), modsT[:, 144:152], modsT[:, 152:160], P.dout("hmixT", [D, TC], BF16))]
    ph_ffn(P, C, T2, modsT, layer, extra)
    return P.build()


def _hs_nsa_methods():
    def gather_batch(self, per_core):
        return [np.ascontiguousarray(np.concatenate([per_core[4 * b + i] for i in range(4)], axis=1)) for b in range(2)]

    def run_nsa(self, li):
        inp = self.inp
        first = (li == 0)
        nc = get_prog("NSA%d" % li, lambda: build_nsa(first))
        if not hasattr(self, "nsac"):
            self.nsac = nsa_host_consts()
        hmix_all = gather_batch(self, self.hmixT)
        if first:
            hkv_all = gather_batch(self, self.hkvT)
            wkv6 = inp["nsa_w_kv"].reshape(D, 6, 4, 64)
            w1 = inp["cmp_w1"]
            w1c = np.ascontiguousarray(np.concatenate(
                [w1[i].reshape(32, 64, 256).transpose(1, 0, 2) for i in range(2)], axis=0))
            posT2 = np.ascontiguousarray(np.repeat(np.concatenate(
                [inp["cmp_pos"][i].T for i in range(2)], axis=0)[:, :, None], 2, axis=2))
            w2k = np.zeros((128, 2, 128), np.float32)
            w2k[:, :, 64:128] = inp["cmp_w2"][0].reshape(2, 128, 64).transpose(1, 0, 2)
            w2v = np.ascontiguousarray(inp["cmp_w2"][1].reshape(2, 128, 64).transpose(1, 0, 2))
            kn = inp["nsa_k_norm"]
            knorm = np.zeros((128, 2), np.float32)
            knorm[:, 0] = np.concatenate([kn[2], kn[1]])
            knorm[64:, 1] = kn[0]
        wq_full = inp["nsa_w_q"][li]
        maps = []
        for c in range(8):
            b, r = c // 4, c % 4
            m = dict(self.nsac)
            m["hmixT_all"] = hmix_all[b]
            m["wq"] = np.ascontiguousarray(wq_full[:, 256 * r:256 * (r + 1)])
            gcols = [1024 + br * 16 + 4 * r + g for br in range(3) for g in range(4)]
            m["wg"] = np.ascontiguousarray(wq_full[:, gcols])
            m["qnorm"] = np.ascontiguousarray(np.tile(inp["nsa_q_norm"][li], 2).reshape(128, 1))
            if first:
                m["hkvT_all"] = hkv_all[b]
                m["wkv"] = np.ascontiguousarray(wkv6[:, [4, 2, 0, 1, 3, 5], r, :].reshape(D, 384))
                m.update(knorm=knorm, w1c=w1c, posT2=posT2, w2k=w2k, w2v=w2v)
            else:
                m.update({"i_" + k: v for k, v in self.kvstore[c].items()})
            maps.append(m)
        res = run_prog(nc, maps)
        if first:
            self.kvstore = [{k: np.asarray(res[c]["o_" + k]) for k in ("Ksel", "Kwin", "Kc", "Vsel", "Vwin", "Vc")}
                            for c in range(8)]
        oT = [np.asarray(res[c]["oTD"]) for c in range(8)]
        self.uT = []
        for c in range(8):
            b, rr = c // 4, c % 4
            parts = [oT[4 * b + r][:, :, rr * TC:(rr + 1) * TC].transpose(1, 0, 2).reshape(256, TC) for r in range(4)]
            self.uT.append(np.ascontiguousarray(np.concatenate(parts, axis=0)))

    def run_L5(self, last):
        inp = self.inp
        li = 1 if last else 0
        layer = 2 + li
        nc = get_prog("L5_%d" % li, lambda: build_L5(last))
        maps = []
        for c in range(8):
            m = dict(modsT_in=self.modsT[c // 4], xT=self.xT[c], uT=self.uT[c], w_out=inp["nsa_w_out"][li],
                     **self._ffn_inputs(layer))
            if not last:
                m["gain_mixn"] = fm(inp["norm_mix"][3])
            maps.append(m)
        res = run_prog(nc, maps)
        self.xT = [np.asarray(res[c]["x2T"]) for c in range(8)]
        if not last:
            self.hmixT = [np.asarray(res[c]["hmixT"]) for c in range(8)]

    HostState.run_nsa = run_nsa
    HostState.run_L5 = run_L5


_hs_nsa_methods()


def kernel(**inputs):
    st = HostState(inputs)
    st.run_L0()
    st.run_L1()
    st.run_L2()
    st.run_L3()
    st.run_nsa(0)
    st.run_L5(False)
    st.run_nsa(1)
    st.run_L5(True)
    out = np.stack([np.concatenate([st.xT[4 * b + r].T for r in range(4)], axis=0) for b in range(2)])
    return np.ascontiguousarray(out.astype(np.float32))
```

```python
import numpy as np
import ml_dtypes
from contextlib import ExitStack
import concourse.bass as bass
import concourse.mybir as mybir
from concourse.bass_utils import run_bass_kernel_spmd

F32 = mybir.dt.float32
BF16 = mybir.dt.bfloat16
AF = mybir.ActivationFunctionType
ALU = mybir.AluOpType
NPBF = ml_dtypes.bfloat16

ENGS = ["pe", "act", "dve", "pool", "sp"]
EPS = 1e-6


class Op:
    __slots__ = ("eng", "fn", "deps", "idx", "dma", "sem", "inc", "val", "incidx", "ph")


class Prog:
    ARENA_BYTES = 206 * 1024
    N_DSEM = 36
    N_CSET = 12

    def __init__(self):
        self.nc = bass.Bass("TRN2", target_bir_lowering=False)
        self.st = ExitStack()
        self.ops = {e: [] for e in ENGS}
        self.last_w = {}
        self.readers = {}
        self.stores = []
        self.pending_dma = []
        self.arena = self.st.enter_context(self.nc.sbuf_tensor("arena", [128, self.ARENA_BYTES // 2], BF16))
        self.psum = self.st.enter_context(self.nc.psum_tensor("psum", [128, 8 * 512], F32))
        self.top = 0
        self.mark = 0
        self.sem_of_key = {}
        self.sem_cnt = [0] * self.N_DSEM
        self.sem_free = list(range(self.N_DSEM))
        self.phase = 0

    def din(self, name, shape, dt=F32):
        return self.nc.dram_tensor(name, list(shape), dt, kind="ExternalInput").ap()

    def dout(self, name, shape, dt=F32):
        return self.nc.dram_tensor(name, list(shape), dt, kind="ExternalOutput").ap()

    def dscratch(self, name, shape, dt=F32):
        return self.nc.dram_tensor(name, list(shape), dt, kind="Internal").ap()

    def alloc(self, shape, dt=F32):
        esz = 4 if dt == F32 else 2
        n = 1
        for d in shape:
            n *= d
        nb = (n * esz + 31) // 32 * 32
        assert self.top + nb <= self.ARENA_BYTES, ("SBUF arena overflow", self.top, nb)
        a = self.arena[:, self.top // 2:(self.top + nb) // 2]
        self.top += nb
        if dt == F32:
            a = a.bitcast(F32)
        a = a[:, 0:n]
        if len(shape) == 2:
            a = a.rearrange("p (a b) -> p a b", a=shape[0])
        elif len(shape) == 3:
            a = a.rearrange("p (a b c) -> p a b c", a=shape[0], b=shape[1])
        return a

    def bank(self, i, dt=F32):
        b = self.psum[:, i * 512:(i + 1) * 512]
        if dt == BF16:
            b = b.bitcast(BF16)
        return b

    def set_mark(self):
        self.mark = self.top

    def new_phase(self):
        self.barrier()
        self.top = self.mark

    def _sem_for(self, key):
        s = self.sem_of_key.get(key)
        if s is None:
            s = self.sem_free.pop(0)
            self.sem_of_key[key] = s
        return s

    def op(self, eng, fn, r=(), w=(), dma=False, semkey=None):
        o = Op()
        o.eng, o.fn, o.dma, o.inc, o.val, o.incidx, o.sem = eng, fn, dma, False, 0, 0, None
        o.ph = self.phase
        if dma:
            sk = semkey if semkey is not None else (w[0] if w else r[0])
            o.sem = self._sem_for(sk)
            self.sem_cnt[o.sem] += 16
            o.val = self.sem_cnt[o.sem]
            self.pending_dma.append(o)
        deps = []
        for k in r:
            lw = self.last_w.get(k)
            if lw is not None:
                deps.append(lw)
        for k in w:
            lw = self.last_w.get(k)
            if lw is not None and not (dma and lw.dma and lw.sem == o.sem):
                deps.append(lw)
            deps.extend(self.readers.get(k, ()))
        o.deps = [d for d in deps if d is not o]
        for k in r:
            self.readers.setdefault(k, []).append(o)
        for k in w:
            self.last_w[k] = o
            self.readers[k] = []
        o.idx = len(self.ops[eng])
        self.ops[eng].append(o)
        return o

    def barrier(self):
        tails = [self.ops[e][-1] for e in ENGS if self.ops[e]] + list(self.pending_dma)
        for e in ENGS:
            o = Op()
            o.eng, o.fn, o.dma, o.inc, o.val, o.incidx, o.sem = e, None, False, False, 0, 0, None
            o.ph = self.phase
            o.deps = [d for d in tails]
            o.idx = len(self.ops[e])
            self.ops[e].append(o)
        self.pending_dma = []
        self.last_w = {}
        self.readers = {}
        self.phase += 1
        for k, s in self.sem_of_key.items():
            self.sem_free.append(s)
        self.sem_of_key = {}

    def mm(self, out, lhsT, rhs, start, stop, r, w):
        return self.op("pe", lambda e: e.matmul(out, lhsT, rhs, start=start, stop=stop), r=r, w=w)

    def tr(self, out, in_, ident, r, w):
        return self.op("pe", lambda e: e.transpose(out, in_, ident), r=r, w=w)

    def act(self, out, in_, func, r, w, bias=None, scale=None):
        kw = {}
        if bias is not None:
            kw["bias"] = bias
        if scale is not None:
            kw["scale"] = scale
        return self.op("act", lambda e: e.activation(out, in_, func, **kw), r=r, w=w)

    def tt(self, out, a, b, op, r, w, eng="dve"):
        return self.op(eng, lambda e: e.tensor_tensor(out, a, b, op), r=r, w=w)

    def ts(self, out, a, s1, s2, op0, op1, r, w, eng="dve"):
        if op1 is None:
            return self.op(eng, lambda e: e.tensor_scalar(out, a, s1, None, op0), r=r, w=w)
        return self.op(eng, lambda e: e.tensor_scalar(out, a, s1, s2, op0, op1), r=r, w=w)

    def stt(self, out, in0, scalar, in1, op0, op1, r, w, eng="dve"):
        return self.op(eng, lambda e: e.scalar_tensor_tensor(out, in0, scalar, in1, op0, op1), r=r, w=w)

    def copy(self, out, in_, r, w, eng="dve"):
        if eng == "act":
            return self.op("act", lambda e: e.copy(out, in_), r=r, w=w)
        return self.op(eng, lambda e: e.tensor_copy(out, in_), r=r, w=w)

    def memset(self, ap, val, w, eng="pool"):
        return self.op(eng, lambda e: e.memset(ap, val), r=(), w=w)

    def load(self, eng, out, in_, w, r=(), semkey=None):
        return self.op(eng, lambda e: e.dma_start(out=out, in_=in_), r=r, w=w, dma=True, semkey=semkey)

    def store(self, eng, out, in_, r, w=(), semkey=None, final=True):
        o = self.op(eng, lambda e: e.dma_start(out=out, in_=in_), r=r, w=w, dma=True, semkey=semkey)
        if final:
            self.stores.append(o)
        return o

    def _plan(self):
        plan = {}
        for e in ENGS:
            seen_c = {f: -1 for f in ENGS}
            seen_d = {}
            lst = []
            for o in self.ops[e]:
                cmax = {}
                dmax = {}
                for d in o.deps:
                    if d.dma:
                        if d.val > dmax.get(d.sem, (0, None))[0]:
                            dmax[d.sem] = (d.val, d)
                    else:
                        if d.fn is None:
                            continue
                        if d.eng == e and e == "pe":
                            continue
                        if d.idx > cmax.get(d.eng, (-1, None))[0]:
                            cmax[d.eng] = (d.idx, d)
                waits = []
                for f, (idx, d) in cmax.items():
                    if seen_c[f] < idx:
                        seen_c[f] = idx
                        d.inc = True
                        waits.append(d)
                for sk, (val, d) in dmax.items():
                    if seen_d.get(sk, 0) < val:
                        seen_d[sk] = val
                        waits.append(d)
                lst.append((o, waits))
            plan[e] = lst
        return plan

    def build(self):
        nc = self.nc
        self.barrier()
        plan = self._plan()
        self.n_instr = {e: len(self.ops[e]) for e in ENGS}
        for e in ENGS:
            cnt = [0] * self.N_CSET
            for o in self.ops[e]:
                if o.inc and not o.dma:
                    cnt[o.ph % self.N_CSET] += 1
                    o.incidx = cnt[o.ph % self.N_CSET]
        csem = {e: [self.st.enter_context(nc.semaphore("c_%s%d" % (e, i))) for i in range(self.N_CSET)]
                for e in ENGS}
        dsem = [self.st.enter_context(nc.semaphore("d%d" % i)) for i in range(self.N_DSEM)]

        def emit(ename, E):
            for o, waits in plan[ename]:
                for d in waits:
                    if d.dma:
                        E.wait_ge(dsem[d.sem], d.val)
                    else:
                        E.wait_ge(csem[d.eng][d.ph % self.N_CSET], d.incidx)
                if o.fn is None:
                    continue
                ins = o.fn(E)
                if o.dma:
                    ins.then_inc(dsem[o.sem], 16)
                elif o.inc:
                    ins.then_inc(csem[ename][o.ph % self.N_CSET], 1)

        with nc.Block() as block:
            @block.tensor
            def _(E):
                emit("pe", E)

            @block.scalar
            def _(E):
                emit("act", E)

            @block.vector
            def _(E):
                emit("dve", E)

            @block.gpsimd
            def _(E):
                emit("pool", E)

            @block.sync
            def _(E):
                emit("sp", E)
        self.st.close()
        return nc


def run_prog(prog_nc, in_maps):
    if DBG.get("one_core"):
        r = run_bass_kernel_spmd(prog_nc, [in_maps[0]], core_ids=[0]).results
        return [r[0]] * len(in_maps)
    res = run_bass_kernel_spmd(prog_nc, in_maps, core_ids=list(range(8)))
    return res.results


D = 1024
KT = 8
S_LEN = 16384
TC = 4096
TT = 512
NTT = TC // TT
FH = 2816
FJ = FH // 128
NMOD = 8 * 24 + 16
DBG = {}


def fm(v):
    v = np.asarray(v)
    return np.ascontiguousarray(v.reshape(-1, 128).T)


class Consts:
    pass


def setup_consts(P, cin):
    C = Consts()
    C.ones_d = P.alloc([128], BF16)
    C.ones_128 = P.alloc([128], BF16)
    C.eps = P.alloc([1])
    C.one = P.alloc([1])
    P.memset(C.ones_d, 1.0 / D, w=["c_ones_d"])
    P.memset(C.ones_128, 1.0 / 128, w=["c_ones_128"])
    P.memset(C.eps, EPS, w=["c_eps"])
    P.memset(C.one, 1.0, w=["c_one"])
    return C


def load_cast_weight(P, w_dram, w_sb, rows_kt, ncols, stage, key, chunk, eng="sp"):
    i = 0
    for kt in range(rows_kt):
        for c0 in range(0, ncols, chunk):
            c1 = min(ncols, c0 + chunk)
            sg = stage[i % len(stage)]
            skey = ("stage", i % len(stage))
            P.load(eng, sg[:, 0:c1 - c0], w_dram[kt * 128:(kt + 1) * 128, c0:c1], w=[skey])
            P.copy(w_sb[:, kt, c0:c1], sg[:, 0:c1 - c0], r=[skey], w=[key], eng="pool")
            i += 1


def rms_rstd(P, C, src, src_keys, nk, sq, sq_keys, ones, ones_key, ps_ss, ps_key, rstd, rstd_key, ncol):
    P.act(sq[:, 0:nk, 0:ncol], src[:, 0:nk, 0:ncol], AF.Square, r=list(src_keys), w=list(sq_keys))
    for k in range(nk):
        P.mm(ps_ss[:, 0:ncol], ones, sq[:, k, 0:ncol], k == 0, k == nk - 1,
             r=[sq_keys[k], ones_key], w=[ps_key])
    P.act(rstd[:, 0:ncol], ps_ss[:, 0:ncol], AF.Sqrt, r=[ps_key, "c_eps"], w=[rstd_key], bias=C.eps[:, 0:1], scale=1.0)
    P.op("dve", lambda e: e.reciprocal(rstd[:, 0:ncol], rstd[:, 0:ncol]), r=[rstd_key], w=[rstd_key])


def modulate(P, C, xt, xkeys, hT, hkeys, gs, gs_key, shift, shift_key, ps_ss, ps_key, rstd, tmp, ncol=TT):
    rms_rstd(P, C, xt, xkeys, KT, hT, hkeys, C.ones_d, "c_ones_d", ps_ss, ps_key, rstd, "rstd", ncol)
    for k in range(KT):
        t_ = tmp[k % 2]
        tk = ("tmp", k % 2)
        P.stt(t_[:, 0:ncol], xt[:, k, 0:ncol], gs[:, k:k + 1], rstd[:, 0:ncol], ALU.mult, ALU.mult,
              r=[xkeys[k], gs_key, "rstd"], w=[tk])
        P.act(hT[:, k, 0:ncol], t_[:, 0:ncol], AF.Identity, r=[tk, shift_key], w=[hkeys[k]],
              bias=shift[:, k:k + 1], scale=1.0)


def make_gs(P, gs, gs_key, gain, gain_key, scale, scale_key):
    P.stt(gs, scale, 1.0, gain, ALU.add, ALU.mult, r=[gain_key, scale_key], w=[gs_key])


def ph_mods(P, T, mods_out):
    c_sb = P.alloc([8])
    cact2 = P.alloc([8, 2])
    bias = P.alloc([52])
    wb = [P.alloc([8, 768]), P.alloc([8, 768])]
    P.load("sp", c_sb, T["cT"], w=["c_sb"])
    P.load("sp", bias, T["mod_bT"], w=["mbias"])
    P.act(cact2[:, :, 0], c_sb, AF.Silu, r=["c_sb"], w=["cact2"])
    P.act(cact2[:, :, 1], c_sb, AF.Silu, r=["c_sb"], w=["cact2"])
    for s in range(9):
        cw = 768 if s < 8 else 512
        wsrc = T["ada_w"][s] if s < 8 else T["kv_ada_w"]
        wv = wsrc.rearrange("(k p) n -> p k n", p=128)
        ps = P.bank(s % 2)
        pk = ("mps", s % 2)
        buf = wb[s % 2]
        bkey = ("wb", s % 2)
        P.load("sp" if s % 2 == 0 else "act", buf[:, :, 0:cw], wv[:, :, 0:cw], w=[bkey])
        nt = cw // 128
        for n in range(nt):
            for k in range(8):
                P.mm(ps[:, 2 * n:2 * n + 2], buf[:, k, n * 128:(n + 1) * 128], cact2[:, k, :], k == 0, k == 7,
                     r=[bkey, "cact2"], w=[pk])
        off = s * 6
        P.tt(mods_out[:, off:off + nt], ps[:, 0:2 * nt].rearrange("p (n two) -> p n two", two=2)[:, :, 0],
             bias[:, off:off + nt], ALU.add, r=[pk, "mbias"], w=["mods_out"])


def ph_hrec(P, C, T, modsT, layer):
    first = (layer == 0)
    Win = P.alloc([8, 4096], BF16)
    IU = P.alloc([256])
    Lrev = P.alloc([128])
    maskneg = P.alloc([128], BF16)
    rowmask = P.alloc([4])
    gain = P.alloc([8])
    gs = P.alloc([8])
    Sst = [P.alloc([8, 128]) for _ in range(4)]
    Sbf = [P.alloc([8, 128], BF16) for _ in range(4)]
    egp = P.alloc([8])
    egpm = P.alloc([2, 4])
    lbrow = None
    if not first:
        lbrow = P.alloc([1024])
    mark0 = P.top
    stage = [P.alloc([2048]), P.alloc([2048])]
    load_cast_weight(P, T["w_in"], Win, 8, 4096, stage, "Win", 2048)
    P.load("act", IU, T["cIU"], w=["IU"])
    P.load("act", Lrev, T["cLrev"], w=["Lrev"])
    P.load("act", maskneg, T["cmaskneg"], w=["maskneg"])
    P.load("act", rowmask, T["crowmask"], w=["rowmask"])
    P.load("act", gain, T["gainT"], w=["gain"])
    sh_off = 2 * layer * 24
    shift = modsT[:, sh_off:sh_off + 8]
    make_gs(P, gs, "gs", gain, "gain", modsT[:, sh_off + 8:sh_off + 16], "modsT")
    if not first:
        lbr = P.alloc([2, 1024])
        P.load("act", lbr, T["lbraw"], w=["lbr"])
        P.tt(lbrow, lbr[:, 1, :], lbr[:, 0, :], ALU.subtract, r=["lbr"], w=["lbrow"])
        P.act(lbrow, lbrow, AF.Sigmoid, r=["lbrow"], w=["lbrow"])
    P.memset(Sst[0], 0.0, w=["S0_%d" % h for h in range(8)])
    P.memset(Sbf[0], 0.0, w=["Sbf0_%d" % h for h in range(8)])
    P.memset(egp, 1.0, w=["egp%d" % h for h in range(8)])
    P.barrier()
    P.top = mark0
    xt = P.alloc([8, TT])
    xtb = xt.rearrange("p a b -> p (a b)").bitcast(BF16)
    oloc_sb = xtb[:, 0:4096].rearrange("p (a b) -> p a b", a=8)
    qseg_sb = xtb[:, 4096:8192].rearrange("p (a b) -> p a b", a=8)
    hT = P.alloc([8, TT], BF16)
    rstd = P.alloc([TT])
    tmp = [P.alloc([TT]), P.alloc([TT])]
    q_sb = P.alloc([8, TT], BF16)
    sg_sb = P.alloc([8, TT], BF16)
    f_sb = [P.alloc([1024]), P.alloc([1024])]
    v_sb = [P.alloc([1024], BF16), P.alloc([1024], BF16)]
    tA = P.alloc([1024])
    tB = P.alloc([1024])
    tCc = P.alloc([1024])
    kdc = [P.alloc([1024], BF16) for _ in range(4)]
    eG = [P.alloc([128]), P.alloc([128])]
    einv = [P.alloc([128]), P.alloc([128])]
    fT = [P.alloc([128]), P.alloc([128])]
    qg = [P.alloc([128], BF16), P.alloc([128], BF16)]
    kinv = [P.alloc([128], BF16), P.alloc([128], BF16)]
    attnT = [P.alloc([128], BF16), P.alloc([128], BF16)]
    xv = T["xT"].rearrange("(k p) t -> p k t", p=128)
    olv = T["oloc"].rearrange("(k p) t -> p k t", p=128)
    qsv = T["qseg"].rearrange("(k p) t -> p k t", p=128)
    sgv = T["sgT"].rearrange("(k p) t -> p k t", p=128)
    xkeys = ["xt%d" % k for k in range(8)]
    hkeys = [("hT", k) for k in range(8)]
    for tt in range(DBG.get("ntt", NTT)):
        c0 = tt * TT
        P.load("sp", xt, xv[:, :, c0:c0 + TT], w=xkeys)
        modulate(P, C, xt, xkeys, hT, hkeys, gs, "gs", shift, "modsT", P.bank(0), ("b", 0), rstd, tmp)
        for n in range(8):
            b = n % 2
            for k in range(8):
                P.mm(P.bank(b), Win[:, k, n * 128:(n + 1) * 128], hT[:, k, :], k == 0, k == 7,
                     r=["Win", hkeys[k]], w=[("b", b)])
            P.act(q_sb[:, n, :], P.bank(b), AF.Silu, r=[("b", b)], w=[("q", n)])
        for n in range(8):
            b = n % 2
            for k in range(8):
                P.mm(P.bank(b), Win[:, k, 3072 + n * 128:3072 + (n + 1) * 128], hT[:, k, :], k == 0, k == 7,
                     r=["Win", hkeys[k]], w=[("b", b)])
            P.act(sg_sb[:, n, :], P.bank(b), AF.Silu, r=[("b", b)], w=["sg_sb"])
        P.store("sp", sgv[:, :, c0:c0 + TT], sg_sb, r=["sg_sb"])
        for sub in range(DBG.get("nsub", 4)):
            sp_ = sub % 2
            fs, vs = f_sb[sp_], v_sb[sp_]
            fk, vk = ("f", sp_), ("v", sp_)
            tok = slice(sub * 128, (sub + 1) * 128)
            for half in range(2):
                b = 2 + half
                for k in range(8):
                    P.mm(P.bank(b), hT[:, k, tok], Win[:, k, 1024 + half * 512:1024 + (half + 1) * 512],
                         k == 0, k == 7, r=["Win", hkeys[k]], w=[("b", b)])
                P.copy(fs[:, half * 512:(half + 1) * 512], P.bank(b), r=[("b", b)], w=[fk], eng="dve")
            for half in range(2):
                b = 2 + half
                for k in range(8):
                    P.mm(P.bank(b), hT[:, k, tok], Win[:, k, 2048 + half * 512:2048 + (half + 1) * 512],
                         k == 0, k == 7, r=["Win", hkeys[k]], w=[("b", b)])
                P.copy(vs[:, half * 512:(half + 1) * 512], P.bank(b), r=[("b", b)], w=[vk], eng="act")
            P.act(tA, fs, AF.Exp, r=[fk], w=["tA"], scale=-1.0)
            P.act(tB, tA, AF.Ln, r=["tA", "c_one"], w=["tB"], bias=C.one[:, 0:1], scale=1.0)
            if first:
                P.ts(tB, tB, -1.0, None, ALU.mult, None, r=["tB"], w=["tB"])
            else:
                P.tt(tCc, tA, lbrow, ALU.mult, r=["tA", "lbrow"], w=["tC"])
                P.act(tCc, tCc, AF.Ln, r=["tC", "c_one"], w=["tC"], bias=C.one[:, 0:1], scale=1.0)
                P.tt(tB, tCc, tB, ALU.subtract, r=["tC", "tB"], w=["tB"])
            P.act(tCc, tB, AF.Exp, r=["tB"], w=["tC"])
            P.ts(tCc, tCc, -1.0, 1.0, ALU.mult, ALU.add, r=["tC"], w=["tC"])
            for half in range(2):
                b = 2 + half
                P.mm(P.bank(b), Lrev, tB[:, half * 512:(half + 1) * 512], True, True,
                     r=["Lrev", "tB"], w=[("b", b)])
                P.act(tA[:, half * 512:(half + 1) * 512], P.bank(b), AF.Exp, r=[("b", b)], w=["tA"])
            P.tt(tCc, tCc, tA, ALU.mult, r=["tC", "tA"], w=["tC"])
            for c in range(4):
                P.ts(kdc[c], tCc, rowmask[:, c:c + 1], None, ALU.mult, None, r=["tC", "rowmask"], w=[("kdc", c)])
            for h in range(DBG.get("nh", 8)):
                p = h % 2
                hc = slice(h * 128, (h + 1) * 128)
                bA = P.bank(4 + p)
                bB = P.bank(6 + p)
                kG, kA, kO = ("psG", p), ("psA", p), ("psO", p)
                P.mm(bA[:, 0:256], tB[:, hc], IU, True, True, r=["tB", "IU"], w=[kG])
                P.act(eG[p], bA[:, 128:256], AF.Exp, r=[kG], w=[("eG", p)])
                P.act(einv[p], bA[:, 128:256], AF.Exp, r=[kG], w=[("einv", p)], scale=-1.0)
                P.act(fT[p], bA[:, 0:128], AF.Exp, r=[kG], w=[("fT", p)])
                P.tt(qg[p], q_sb[:, h, tok], eG[p], ALU.mult, r=[("q", h), ("eG", p)], w=[("qg", p)])
                P.stt(kinv[p], fT[p], 1.0, einv[p], ALU.subtract, ALU.mult,
                      r=[("fT", p), ("einv", p)], w=[("kinv", p)])
                ek = "egp%d" % h
                cur, curk = egp[:, h:h + 1], ek
                for c in range(4):
                    cs = slice(sub * 128 + 32 * c, sub * 128 + 32 * c + 32)
                    P.stt(qseg_sb[:, h, cs], eG[p][:, 32 * c:32 * c + 32], cur, q_sb[:, h, cs], ALU.mult, ALU.mult,
                          r=[("eG", p), curk, ("q", h)], w=["qseg_sb"] + xkeys)
                    if c < 3:
                        nxt, nxtk = egpm[:, p, c:c + 1], ("egpm", p, c)
                    else:
                        nxt, nxtk = egp[:, h:h + 1], ek
                    P.tt(nxt, cur, eG[p][:, 32 * c + 31:32 * c + 32], ALU.mult, r=[curk, ("eG", p)], w=[nxtk])
                    cur, curk = nxt, nxtk
                P.mm(bA[:, 256:384], kinv[p], qg[p], True, True, r=[("kinv", p), ("qg", p)], w=[kA])
                P.tt(attnT[p], bA[:, 256:384], maskneg, ALU.mult, r=[kA, "maskneg"], w=[("attnT", p)])
                for c in range(4):
                    P.mm(bB[:, 128 * c:128 * c + 128], kdc[c][:, hc], vs[:, hc], True, True,
                         r=[("kdc", c), vk], w=[("psS", p, c)])
                for c in range(4):
                    src, srck = Sst[c], "S%d_%d" % (c, h)
                    dst = Sst[(c + 1) % 4]
                    dstk = "S%d_%d" % ((c + 1) % 4, h)
                    P.stt(dst[:, h, :], src[:, h, :], eG[p][:, 32 * c + 31:32 * c + 32], bB[:, 128 * c:128 * c + 128],
                          ALU.mult, ALU.add, r=[srck, ("eG", p), ("psS", p, c)], w=[dstk])
                    if c < 3:
                        P.copy(Sbf[c + 1][:, h, :], dst[:, h, :], r=[dstk], w=["Sbf%d_%d" % (c + 1, h)], eng="act")
                P.mm(bA[:, 384:512], vs[:, hc], attnT[p], True, False, r=[vk, ("attnT", p)], w=[kO])
                for c in range(4):
                    P.mm(bA[:, 384 + 32 * c:384 + 32 * c + 32], Sbf[c][:, h, :], qg[p][:, 32 * c:32 * c + 32],
                         False, c == 3, r=["Sbf%d_%d" % (c, h), ("qg", p)], w=[kO])
                P.copy(Sbf[0][:, h, :], Sst[0][:, h, :], r=["S0_%d" % h], w=["Sbf0_%d" % h], eng="act")
                P.copy(oloc_sb[:, h, tok], bA[:, 384:512], r=[kO], w=["oloc_sb"] + xkeys, eng="act")
        P.store("sp", olv[:, :, c0:c0 + TT], oloc_sb, r=["oloc_sb"] + xkeys)
        P.store("sp", qsv[:, :, c0:c0 + TT], qseg_sb, r=["qseg_sb"] + xkeys)
    P.store("sp", T["Sloc"], Sst[0], r=["S0_%d" % h for h in range(8)])
    P.store("sp", T["Aseg"], egp, r=["egp%d" % h for h in range(8)])


def ph_oproj(P, C, T, modsT, layer, hgrn):
    Wo = P.alloc([8, 1024], BF16)
    gate = modsT[:, 2 * layer * 24 + 16:2 * layer * 24 + 24]
    if hgrn:
        Sin = P.alloc([8, 128])
        Sin_bf = P.alloc([8, 128], BF16)
        onorm = P.alloc([1])
        oneh = P.alloc([4])
    mark0 = P.top
    stage = [P.alloc([2048]), P.alloc([2048])]
    load_cast_weight(P, T["w_out"], Wo, 8, 1024, stage, "Wo", 1024)
    if hgrn:
        Sall = P.alloc([4, 8, 128])
        Aall = P.alloc([4, 8])
        Tt = P.alloc([8, 128])
        P.load("act", Sall, T["Sall"], w=["Sall"])
        P.load("act", Aall, T["Aall"], w=["Aall"])
        P.load("act", oneh, T["onehot"], w=["oneh"])
        P.load("act", onorm, T["onormT"], w=["onorm"])
        P.memset(Tt, 0.0, w=["Tt"])
        P.memset(Sin, 0.0, w=["Sin"])
        for i in range(3):
            for h in range(8):
                P.stt(Tt[:, h, :], Tt[:, h, :], Aall[:, i, h:h + 1], Sall[:, i, h, :], ALU.mult, ALU.add,
                      r=["Tt", "Aall", "Sall"], w=["Tt"])
            P.stt(Sin, Tt, oneh[:, i + 1:i + 2], Sin, ALU.mult, ALU.add, r=["Tt", "oneh", "Sin"], w=["Sin"])
        P.copy(Sin_bf, Sin, r=["Sin"], w=["Sin_bf"], eng="dve")
    P.barrier()
    P.top = mark0
    xt = P.alloc([8, TT])
    uT = P.alloc([8, TT], BF16)
    xv = T["xT"].rearrange("(k p) t -> p k t", p=128)
    x1v = T["x1T"].rearrange("(k p) t -> p k t", p=128)
    xkeys = ["xt%d" % k for k in range(8)]
    if hgrn:
        ol = P.alloc([8, TT], BF16)
        qs = P.alloc([8, TT], BF16)
        sg = P.alloc([8, TT], BF16)
        o32 = [P.alloc([TT]), P.alloc([TT])]
        sq = [P.alloc([TT], BF16), P.alloc([TT], BF16)]
        rs = [P.alloc([TT]), P.alloc([TT])]
        olv = T["oloc"].rearrange("(k p) t -> p k t", p=128)
        qsv = T["qseg"].rearrange("(k p) t -> p k t", p=128)
        sgv = T["sgT"].rearrange("(k p) t -> p k t", p=128)
    else:
        uv = T["uT"].rearrange("(k p) t -> p k t", p=128)
    for tt in range(NTT):
        c0 = tt * TT
        P.load("sp", xt, xv[:, :, c0:c0 + TT], w=xkeys)
        if hgrn:
            P.load("sp", ol, olv[:, :, c0:c0 + TT], w=["ol"])
            P.load("act", qs, qsv[:, :, c0:c0 + TT], w=["qs"])
            P.load("act", sg, sgv[:, :, c0:c0 + TT], w=["sg"])
            for h in range(8):
                p = h % 2
                P.mm(P.bank(p), Sin_bf[:, h, :], qs[:, h, :], True, True, r=["Sin_bf", "qs"], w=[("b", p)])
                P.tt(o32[p], P.bank(p), ol[:, h, :], ALU.add, r=[("b", p), "ol"], w=[("o32", p)])
                P.act(sq[p], o32[p], AF.Square, r=[("o32", p)], w=[("sq", p)])
                P.mm(P.bank(2 + p), C.ones_128, sq[p], True, True, r=["c_ones_128", ("sq", p)], w=[("b", 2 + p)])
                P.act(rs[p], P.bank(2 + p), AF.Sqrt, r=[("b", 2 + p), "c_eps"], w=[("rs", p)],
                      bias=C.eps[:, 0:1], scale=1.0)
                P.op("dve", lambda e, p=p: e.reciprocal(rs[p], rs[p]), r=[("rs", p)], w=[("rs", p)])
                P.stt(o32[p], o32[p], onorm[:, 0:1], rs[p], ALU.mult, ALU.mult,
                      r=[("o32", p), "onorm", ("rs", p)], w=[("o32", p)])
                P.tt(uT[:, h, :], o32[p], sg[:, h, :], ALU.mult, r=[("o32", p), "sg"], w=[("uT", h)], eng="pool")
        else:
            P.load("act", uT, uv[:, :, c0:c0 + TT], w=[("uT", h) for h in range(8)])
        for n in range(8):
            b = 4 + n % 2
            for h in range(8):
                P.mm(P.bank(b), Wo[:, h, n * 128:(n + 1) * 128], uT[:, h, :], h == 0, h == 7,
                     r=["Wo", ("uT", h)], w=[("b", b)])
            P.stt(xt[:, n, :], P.bank(b), gate[:, n:n + 1], xt[:, n, :], ALU.mult, ALU.add,
                  r=[("b", b), "modsT", xkeys[n]], w=[xkeys[n]])
        P.store("sp", x1v[:, :, c0:c0 + TT], xt, r=xkeys, w=[("x1", tt)], final=False)


def ph_ffn(P, C, T, modsT, layer, extra=()):
    W1 = P.alloc([8, 2 * FH], BF16)
    W2 = P.alloc([FJ, D], BF16)
    gain = P.alloc([8])
    gs = P.alloc([8])
    egain = [P.alloc([8]) for _ in extra]
    egs = [P.alloc([8]) for _ in extra]
    mark0 = P.top
    stage = [P.alloc([1408]), P.alloc([1408])]
    load_cast_weight(P, T["w1"], W1, 8, 2 * FH, stage, "W1", 1408)
    load_cast_weight(P, T["w2"], W2, FJ, D, stage, "W2", 1024)
    off = (2 * layer + 1) * 24
    shift = modsT[:, off:off + 8]
    gate = modsT[:, off + 16:off + 24]
    P.load("act", gain, T["gainT"], w=["gain"])
    make_gs(P, gs, "gs", gain, "gain", modsT[:, off + 8:off + 16], "modsT")
    for i, (gd, sh, sc, od) in enumerate(extra):
        P.load("act", egain[i], gd, w=[("egain", i)])
        make_gs(P, egs[i], ("egs", i), egain[i], ("egain", i), sc, "modsT")
    P.barrier()
    P.top = mark0
    xt = P.alloc([8, TT])
    hT = P.alloc([8, TT], BF16)
    gT = P.alloc([FJ, TT], BF16)
    rstd = P.alloc([TT])
    tmp = [P.alloc([TT]), P.alloc([TT])]
    sa = [P.alloc([TT]), P.alloc([TT])]
    xv = T["x1T"].rearrange("(k p) t -> p k t", p=128)
    yv = T["x2T"].rearrange("(k p) t -> p k t", p=128)
    xkeys = ["xt%d" % k for k in range(8)]
    hkeys = [("hT", k) for k in range(8)]
    for tt in range(NTT):
        c0 = tt * TT
        P.load("sp", xt, xv[:, :, c0:c0 + TT], r=[("x1", tt)], w=xkeys)
        modulate(P, C, xt, xkeys, hT, hkeys, gs, "gs", shift, "modsT", P.bank(0), ("b", 0), rstd, tmp)
        for j in range(FJ):
            ba, bb = 1 + j % 2, 3 + j % 2
            for k in range(KT):
                P.mm(P.bank(ba), W1[:, k, j * 128:(j + 1) * 128], hT[:, k, :], k == 0, k == KT - 1,
                     r=["W1", hkeys[k]], w=[("b", ba)])
            for k in range(KT):
                P.mm(P.bank(bb), W1[:, k, FH + j * 128:FH + (j + 1) * 128], hT[:, k, :], k == 0, k == KT - 1,
                     r=["W1", hkeys[k]], w=[("b", bb)])
            P.act(sa[j % 2], P.bank(ba), AF.Silu, r=[("b", ba)], w=[("sa", j % 2)])
            P.tt(gT[:, j, :], sa[j % 2], P.bank(bb), ALU.mult, r=[("sa", j % 2), ("b", bb)], w=[("gT", j)])
        for n in range(KT):
            bo = 5 + n % 2
            for j in range(FJ):
                P.mm(P.bank(bo), W2[:, j, n * 128:(n + 1) * 128], gT[:, j, :], j == 0, j == FJ - 1,
                     r=["W2", ("gT", j)], w=[("b", bo)])
            P.stt(xt[:, n, :], P.bank(bo), gate[:, n:n + 1], xt[:, n, :], ALU.mult, ALU.add,
                  r=[("b", bo), "modsT", xkeys[n]], w=[xkeys[n]])
        P.store("sp", yv[:, :, c0:c0 + TT], xt, r=xkeys)
        for i, (gd, sh, sc, od) in enumerate(extra):
            modulate(P, C, xt, xkeys, hT, hkeys, egs[i], ("egs", i), sh, "modsT", P.bank(0), ("b", 0), rstd, tmp)
            ov = od.rearrange("(k p) t -> p k t", p=128)
            P.store("sp", ov[:, :, c0:c0 + TT], hT, r=hkeys)


def host_consts():
    idx = np.arange(128)
    same = (idx[:, None] // 32) == (idx[None, :] // 32)
    IU = np.zeros((128, 256), np.float32)
    IU[:, :128] = np.eye(128)
    IU[:, 128:] = (same & (idx[:, None] <= idx[None, :])).astype(np.float32)
    Lrev = (same & (idx[:, None] > idx[None, :])).astype(np.float32)
    maskneg = (-(same & (idx[None, :] >= idx[:, None])).astype(np.float32)).astype(NPBF)
    rowmask = (idx[:, None] // 32 == np.arange(4)[None, :]).astype(np.float32)
    return dict(cIU=IU, cLrev=np.ascontiguousarray(Lrev), cmaskneg=np.ascontiguousarray(maskneg),
                crowmask=np.ascontiguousarray(rowmask))


def _hrec_io(P, T, lyr):
    T["w_in"] = P.din("w_in", [D, 4096])
    T["gainT"] = P.din("gain_mix", [128, 8])
    T["cIU"] = P.din("cIU", [128, 256])
    T["cLrev"] = P.din("cLrev", [128, 128])
    T["cmaskneg"] = P.din("cmaskneg", [128, 128], BF16)
    T["crowmask"] = P.din("crowmask", [128, 4])
    if lyr > 0:
        T["lbraw"] = P.din("lbraw", [128, 2, 1024])
    T["oloc"] = P.dout("oloc_o", [D, TC], BF16)
    T["qseg"] = P.dout("qseg_o", [D, TC], BF16)
    T["sgT"] = P.dout("sgT_o", [D, TC], BF16)
    T["Sloc"] = P.dout("Sloc_o", [128, 8, 128])
    T["Aseg"] = P.dout("Aseg_o", [128, 8])


def _oproj_hgrn_io(P, T):
    T["oloc"] = P.din("oloc", [D, TC], BF16)
    T["qseg"] = P.din("qseg", [D, TC], BF16)
    T["sgT"] = P.din("sgT", [D, TC], BF16)
    T["Sall"] = P.din("Sall", [128, 4, 8, 128])
    T["Aall"] = P.din("Aall", [128, 4, 8])
    T["onehot"] = P.din("onehot", [128, 4])
    T["onormT"] = P.din("onormT", [128, 1])
    T["w_out"] = P.din("w_out", [D, D])


def _ffn_io(P, T):
    T["w1"] = P.din("w1", [D, 2 * FH])
    T["w2"] = P.din("w2", [FH, D])
    T["gainT"] = P.din("gain_ffn", [128, 8])


def _start(P, load_mods=True):
    modsT = P.alloc([NMOD])
    C = setup_consts(P, None)
    if load_mods:
        mi = P.din("modsT_in", [128, NMOD])
        P.load("sp", modsT, mi, w=["modsT"])
    P.set_mark()
    return modsT, C


def build_L0():
    P = Prog()
    P.set_mark()
    T = dict(cT=P.din("cT", [128, 8]), ada_w=P.din("ada_w", [8, D, 768]), mod_bT=P.din("mod_bT", [128, 52]),
             kv_ada_w=P.din("kv_ada_w", [D, 512]))
    mo = P.dout("mods_o", [128, 52])
    msb = P.alloc([52])
    ph_mods(P, T, msb)
    P.store("sp", mo, msb, r=["mods_out"])
    return P.build()


def build_L1():
    P = Prog()
    modsT, C = _start(P)
    T2 = dict(xT=P.din("xT", [D, TC]))
    _hrec_io(P, T2, 0)
    ph_hrec(P, C, T2, modsT, 0)
    return P.build()


def build_L2():
    P = Prog()
    modsT, C = _start(P)
    T = dict(xT=P.din("xT", [D, TC]), x1T=P.dscratch("x1T", [D, TC]))
    _oproj_hgrn_io(P, T)
    ph_oproj(P, C, T, modsT, 0, True)
    P.new_phase()
    T2 = dict(x1T=T["x1T"], x2T=P.dout("x2T", [D, TC]))
    _ffn_io(P, T2)
    ph_ffn(P, C, T2, modsT, 0)
    P.new_phase()
    T3 = dict(xT=T2["x2T"])
    _hrec_io(P, T3, 1)
    ph_hrec(P, C, T3, modsT, 1)
    return P.build()


def build_L3():
    P = Prog()
    modsT, C = _start(P)
    T = dict(xT=P.din("xT", [D, TC]), x1T=P.dscratch("x1T", [D, TC]))
    _oproj_hgrn_io(P, T)
    ph_oproj(P, C, T, modsT, 1, True)
    P.new_phase()
    T2 = dict(x1T=T["x1T"], x2T=P.dout("x2T", [D, TC]))
    _ffn_io(P, T2)
    extra = [(P.din("gain_kv", [128, 8]), modsT[:, 192:200], modsT[:, 200:208], P.dout("hkvT", [D, TC], BF16)),
             (P.din("gain_mixn", [128, 8]), modsT[:, 96:104], modsT[:, 104:112], P.dout("hmixT", [D, TC], BF16))]
    ph_ffn(P, C, T2, modsT, 1, extra)
    return P.build()


_PROGS = {}


def get_prog(name, builder):
    if name not in _PROGS:
        _PROGS[name] = builder()
    return _PROGS[name]


class HostState:
    def __init__(self, inp):
        self.inp = {k: np.asarray(v) for k, v in inp.items()}
        self.hc = host_consts()
        x = self.inp["x"]
        self.xT = [np.ascontiguousarray(x[c // 4, (c % 4) * TC:(c % 4 + 1) * TC, :].T) for c in range(8)]

    def run_L0(self):
        inp = self.inp
        nc = get_prog("L0", build_L0)
        maps = []
        for c in range(8):
            b, r = c // 4, c % 4
            mod_b = np.zeros((128, 52), np.float32)
            for s in range(8):
                mod_b[:, s * 6:(s + 1) * 6] = inp["ada_b"][s, 768 * r:768 * (r + 1)].reshape(6, 128).T
            mod_b[:, 48:52] = inp["kv_ada_b"][512 * r:512 * (r + 1)].reshape(4, 128).T
            maps.append(dict(cT=fm(inp["c"][b]),
                             ada_w=np.ascontiguousarray(inp["ada_w"][:, :, 768 * r:768 * (r + 1)]),
                             mod_bT=mod_b,
                             kv_ada_w=np.ascontiguousarray(inp["kv_ada_w"][:, 512 * r:512 * (r + 1)])))
        res = run_prog(nc, maps)
        self.modsT = []
        for b in range(2):
            m = np.zeros((128, NMOD), np.float32)
            for r in range(4):
                o = np.asarray(res[4 * b + r]["mods_o"])
                for s in range(8):
                    m[:, s * 24 + 6 * r:s * 24 + 6 * r + 6] = o[:, s * 6:(s + 1) * 6]
                m[:, 192 + 4 * r:192 + 4 * r + 4] = o[:, 48:52]
            self.modsT.append(m)

    def _hrec_inputs(self, layer):
        inp = self.inp
        d = dict(w_in=inp["hgrn_w_in"][layer], gain_mix=fm(inp["norm_mix"][layer]), **self.hc)
        if layer > 0:
            d["lbraw"] = np.ascontiguousarray(np.broadcast_to(inp["hgrn_lower_bounds"][None], (128, 2, 1024)))
        return d

    def _take_hrec(self, res):
        self.hrec = [{k: np.asarray(res[c][k + "_o"]) for k in ("oloc", "qseg", "sgT", "Sloc", "Aseg")}
                     for c in range(8)]

    def _oproj_hgrn_inputs(self, c, layer):
        inp = self.inp
        b, r = c // 4, c % 4
        h = self.hrec[c]
        oneh = np.zeros((128, 4), np.float32)
        oneh[:, r] = 1.0
        return dict(oloc=h["oloc"], qseg=h["qseg"], sgT=h["sgT"],
                    Sall=np.ascontiguousarray(np.stack([self.hrec[4 * b + i]["Sloc"] for i in range(4)], axis=1)),
                    Aall=np.ascontiguousarray(np.stack([self.hrec[4 * b + i]["Aseg"] for i in range(4)], axis=1)),
                    onehot=oneh, onormT=np.ascontiguousarray(inp["hgrn_out_norm"][layer].reshape(128, 1)),
                    w_out=inp["hgrn_w_out"][layer])

    def _ffn_inputs(self, layer):
        inp = self.inp
        return dict(w1=inp["ffn_w_in"][layer], w2=inp["ffn_w_out"][layer], gain_ffn=fm(inp["norm_ffn"][layer]))

    def run_L1(self):
        nc = get_prog("L1", build_L1)
        maps = [dict(modsT_in=self.modsT[c // 4], xT=self.xT[c], **self._hrec_inputs(0)) for c in range(8)]
        self._take_hrec(run_prog(nc, maps))

    def run_L2(self):
        nc = get_prog("L2", build_L2)
        maps = [dict(modsT_in=self.modsT[c // 4], xT=self.xT[c], **self._oproj_hgrn_inputs(c, 0),
                     **self._ffn_inputs(0), **self._hrec_inputs(1)) for c in range(8)]
        res = run_prog(nc, maps)
        self.xT = [np.asarray(res[c]["x2T"]) for c in range(8)]
        self._take_hrec(res)

    def run_L3(self):
        inp = self.inp
        nc = get_prog("L3", build_L3)
        maps = [dict(modsT_in=self.modsT[c // 4], xT=self.xT[c], **self._oproj_hgrn_inputs(c, 1),
                     **self._ffn_inputs(1), gain_kv=fm(inp["kv_norm"]), gain_mixn=fm(inp["norm_mix"][2]))
                for c in range(8)]
        res = run_prog(nc, maps)
        self.xT = [np.asarray(res[c]["x2T"]) for c in range(8)]
        self.hkvT = [np.asarray(res[c]["hkvT"]) for c in range(8)]
        self.hmixT = [np.asarray(res[c]["hmixT"]) for c in range(8)]


NQT = S_LEN // 128
NEGB = -30000.0


def nsa_host_consts():
    idx = np.arange(128)
    c = {}
    c["ident"] = np.eye(128, dtype=np.float32).astype(NPBF)
    blk = ((idx[:, None] // 64) == (idx[None, :] // 64)).astype(np.float32) / 64.0
    c["blk64"] = blk.astype(NPBF)
    RT = np.zeros((128, 128), np.float32)
    for base in (0, 64):
        for m in range(32):
            RT[base + m + 32, base + m] = -1.0
            RT[base + m + 32 - 32 + 0, base + m + 32] = 1.0
    c["RT2"] = RT.astype(NPBF)
    inv_freq = 1.0 / (10000.0 ** (np.arange(0, 64, 2, dtype=np.float32) / 64))
    ang = np.arange(S_LEN, dtype=np.float32)[None, :] * inv_freq[:, None]
    cos = np.cos(ang).astype(np.float32)
    sin = np.sin(ang).astype(np.float32)
    c["cosT"] = np.ascontiguousarray(np.tile(cos, (4, 1)))
    c["sinT"] = np.ascontiguousarray(np.tile(sin, (4, 1)))
    c["Dm"] = (16.0 * idx[:, None] + 31.0 - idx[None, :]).astype(np.float32)
    cs = np.arange(1024) * 16
    ss = np.arange(256) * 64
    ov = ((cs[:, None] < ss[None, :] + 64) & (cs[:, None] + 32 > ss[None, :])).astype(np.float32)
    ov[1023, :] = 0.0
    c["ov"] = np.ascontiguousarray(ov.reshape(8, 128, 256).transpose(1, 0, 2))
    c["tri01"] = (idx[:, None] <= idx[None, :]).astype(np.float32).astype(NPBF)
    c["low01"] = (idx[:, None] > idx[None, :]).astype(np.float32).astype(NPBF)
    keys = np.arange(S_LEN)
    Z = (((keys[None, :] // 64) % 64) == np.arange(64)[:, None]).astype(np.float32)
    c["Zpat"] = Z.astype(NPBF)
    SelG = np.zeros((12, 12, 64), np.float32)
    for i in range(12):
        SelG[i, i, :] = 1.0
    c["SelG"] = SelG
    Sh = np.zeros((128, 64), np.float32)
    Sh[64, :] = 1.0
    c["ShiftB"] = Sh
    return c


def head_norm_rope(P, C, N, ps, pkey, gvec, gkey, cosd, sind, c0, ncol, bank_n, bank_r, W, tag):
    P.act(W["sq"][:, 0:ncol], ps[:, 0:ncol], AF.Square, r=[pkey], w=["n_sq"])
    P.mm(P.bank(bank_n)[:, 0:ncol], N["blk64"], W["sq"][:, 0:ncol], True, True, r=["n_sq", "blk64"], w=[("b", bank_n)])
    P.act(W["rs"][:, 0:ncol], P.bank(bank_n)[:, 0:ncol], AF.Sqrt, r=[("b", bank_n), "c_eps"], w=["n_rs"],
          bias=C.eps[:, 0:1], scale=1.0)
    P.op("dve", lambda e: e.reciprocal(W["rs"][:, 0:ncol], W["rs"][:, 0:ncol]), r=["n_rs"], w=["n_rs"])
    P.stt(W["kn"][:, 0:ncol], ps[:, 0:ncol], gvec, W["rs"][:, 0:ncol], ALU.mult, ALU.mult,
          r=[pkey, gkey, "n_rs"], w=["n_kn"])
    P.copy(W["knb"][:, 0:ncol], W["kn"][:, 0:ncol], r=["n_kn"], w=["n_knb"], eng="act")
    P.mm(P.bank(bank_r)[:, 0:ncol], N["RT2"], W["knb"][:, 0:ncol], True, True, r=["n_knb", "RT2"], w=[("b", bank_r)])
    P.load("sp", W["cos"][:, 0:ncol], cosd[:, c0:c0 + ncol], w=["n_cos"])
    P.load("sp", W["sin"][:, 0:ncol], sind[:, c0:c0 + ncol], w=["n_sin"])
    P.tt(W["t1"][:, 0:ncol], W["kn"][:, 0:ncol], W["cos"][:, 0:ncol], ALU.mult, r=["n_kn", "n_cos"], w=["n_t1"])
    P.tt(W["t2"][:, 0:ncol], P.bank(bank_r)[:, 0:ncol], W["sin"][:, 0:ncol], ALU.mult, r=[("b", bank_r), "n_sin"], w=["n_t2"])


def alloc_norm_work(P):
    return dict(sq=P.alloc([TT], BF16), rs=P.alloc([TT]), kn=P.alloc([TT]), knb=P.alloc([TT], BF16),
                cos=P.alloc([TT]), sin=P.alloc([TT]), t1=P.alloc([TT]), t2=P.alloc([TT]))


def nsa_load_consts(P, T):
    N = {}
    for nm, shp, dt in (("ident", [128], BF16), ("blk64", [128], BF16), ("RT2", [128], BF16), ("Dm", [128], F32),
                        ("ov", [8, 256], F32), ("tri01", [128], BF16), ("low01", [128], BF16),
                        ("ShiftB", [64], F32)):
        N[nm] = P.alloc(shp, dt)
        P.load("act", N[nm], T[nm], w=[nm])
    N["SelG"] = P.alloc([12, 64])
    P.load("act", N["SelG"][0:12, :, :], T["SelG"], w=["SelG"])
    N["ones_c"] = P.alloc([128], BF16)
    P.memset(N["ones_c"], 1.0, w=["ones_c"])
    return N


def nsa_const_io(P, T):
    T["ident"] = P.din("ident", [128, 128], BF16)
    T["blk64"] = P.din("blk64", [128, 128], BF16)
    T["RT2"] = P.din("RT2", [128, 128], BF16)
    T["Dm"] = P.din("Dm", [128, 128])
    T["ov"] = P.din("ov", [128, 8, 256])
    T["tri01"] = P.din("tri01", [128, 128], BF16)
    T["low01"] = P.din("low01", [128, 128], BF16)
    T["ShiftB"] = P.din("ShiftB", [128, 64])
    T["SelG"] = P.din("SelG", [12, 12, 64])
    T["cosT"] = P.din("cosT", [128, S_LEN])
    T["sinT"] = P.din("sinT", [128, S_LEN])
    T["Zpat"] = P.din("Zpat", [64, S_LEN], BF16)


def alloc_kv(P):
    KV = {}
    KV["KselE"] = P.alloc([S_LEN], BF16)
    KV["KwinT"] = P.alloc([S_LEN], BF16)
    KV["Vsel"] = P.alloc([128, 66], BF16)
    KV["Vwin"] = P.alloc([128, 66], BF16)
    KV["Vc"] = P.alloc([8, 64], BF16)
    return KV


def ph_kvp(P, C, N, T, KV):
    Wkv = P.alloc([8, 384], BF16)
    knorm = P.alloc([2])
    KVraw = P.alloc([S_LEN + 16], BF16)
    mark1 = P.top
    stage = [P.alloc([2048]), P.alloc([2048])]
    load_cast_weight(P, T["wkv"], Wkv, 8, 384, stage, "Wkv", 384)
    P.load("act", knorm, T["knorm"], w=["knorm"])
    P.load("act", KV["KselE"][0:64, :], T["Zpat"], w=["KselE_z"])
    P.memset(KVraw[:, S_LEN:S_LEN + 16], 0.0, w=["KVraw_tail"])
    P.memset(KV["Vsel"][:, :, 64:66], 1.0, w=["Vsel_ones"])
    P.memset(KV["Vwin"][:, :, 64:66], 1.0, w=["Vwin_ones"])
    P.barrier()
    P.top = mark1
    hk = [P.alloc([8, TT], BF16), P.alloc([8, TT], BF16)]
    W = alloc_norm_work(P)
    hv = T["hkvT_all"].rearrange("(k p) t -> p k t", p=128)
    for tt in range(DBG.get("kvp_ntt", S_LEN // TT)):
        c0 = tt * TT
        h_ = hk[tt % 2]
        hkey = ("hk", tt % 2)
        P.load("sp", h_, hv[:, :, c0:c0 + TT], w=[hkey])
        part = DBG.get("kvp_part", 15)
        if part & 1:
            for k in range(8):
                P.mm(P.bank(0), Wkv[:, k, 0:128], h_[:, k, :], k == 0, k == 7, r=["Wkv", hkey], w=[("b", 0)])
            head_norm_rope(P, C, N, P.bank(0), ("b", 0), knorm[:, 0:1], "knorm", T["cosT"], T["sinT"], c0, TT, 1, 2, W, "k")
        if part & 8:
            P.tt(KV["KwinT"][0:64, c0:c0 + TT], W["t1"][0:64, :], W["t2"][0:64, :], ALU.add, r=["n_t1", "n_t2"], w=["Kwin"])
            P.tt(KV["KselE"][64:128, c0:c0 + TT], W["t1"][64:128, :], W["t2"][64:128, :], ALU.add,
                 r=["n_t1", "n_t2"], w=["Ksel"])
        if part & 2:
            for k in range(8):
                P.mm(P.bank(3), Wkv[:, k, 128:256], h_[:, k, :], k == 0, k == 7, r=["Wkv", hkey], w=[("b", 3)])
            P.copy(KVraw[:, c0:c0 + TT], P.bank(3), r=[("b", 3)], w=["KVraw"], eng="act")
        if DBG.get("kvp_bar", False):
            P.barrier()
    P.barrier()
    for tt in range(0 if DBG.get("kvp_skip_v") else DBG.get("kvp_ntt", S_LEN // TT)):
        c0 = tt * TT
        h_ = hk[tt % 2]
        hkey = ("hk", tt % 2)
        P.load("sp", h_, hv[:, :, c0:c0 + TT], w=[hkey])
        for sub in range(4):
            b = DBG.get("vbank", 4) + sub % 2
            kt = tt * 4 + sub
            for k in range(8):
                P.mm(P.bank(b)[:, 0:128], h_[:, k, sub * 128:(sub + 1) * 128], Wkv[:, k, 256:384], k == 0, k == 7,
                     r=["Wkv", hkey], w=[("b", b)])
            if DBG.get("v_scratch"):
                P.copy(W["knb"][:, 0:64], P.bank(b)[:, 0:64], r=[("b", b)], w=["Vsel"], eng="act")
                P.copy(W["knb"][:, 64:128], P.bank(b)[:, 64:128], r=[("b", b)], w=["Vwin"], eng="dve")
            elif DBG.get("v_dveonly", True):
                P.copy(KV["Vsel"][:, kt, 0:64], P.bank(b)[:, 0:64], r=[("b", b)], w=["Vsel"], eng="dve")
                P.copy(KV["Vwin"][:, kt, 0:64], P.bank(b)[:, 64:128], r=[("b", b)], w=["Vwin"], eng="dve")
            else:
                P.copy(KV["Vsel"][:, kt, 0:64], P.bank(b)[:, 0:64], r=[("b", b)], w=["Vsel"], eng="act")
                P.copy(KV["Vwin"][:, kt, 0:64], P.bank(b)[:, 64:128], r=[("b", b)], w=["Vwin"], eng="dve")
    P.barrier()
    P.top = mark1
    if DBG.get("kvp_stage", 9) < 1:
        return
    W1c = P.alloc([32, 256], BF16)
    W2k = P.alloc([2, 128], BF16)
    W2v = P.alloc([2, 64], BF16)
    posT = P.alloc([32, 2], BF16)
    posw = P.alloc([4])
    hid = [P.alloc([2, 1024], BF16), P.alloc([2, 1024], BF16)]
    sq = P.alloc([TT], BF16)
    rs = P.alloc([TT])
    stage = [P.alloc([2048]), P.alloc([2048])]
    for i in range(4):
        sg = stage[i % 2]
        skey = ("stage", i % 2)
        P.load("sp", sg, T["w1c"][:, 8 * i:8 * i + 8, :].rearrange("p a b -> p (a b)"), w=[skey])
        P.copy(W1c[:, 8 * i:8 * i + 8, :].rearrange("p a b -> p (a b)"), sg, r=[skey], w=["W1c"], eng="pool")
    s2 = P.alloc([256 + 128 + 64])
    P.load("act", s2[:, 0:256], T["w2k"].rearrange("p a b -> p (a b)"), w=["s2"])
    P.load("act", s2[:, 256:384], T["w2v"].rearrange("p a b -> p (a b)"), w=["s2"])
    P.load("act", s2[:, 384:448], T["posT2"].rearrange("p a b -> p (a b)"), w=["s2"])
    P.copy(W2k.rearrange("p a b -> p (a b)"), s2[:, 0:256], r=["s2"], w=["W2k"], eng="pool")
    P.copy(W2v.rearrange("p a b -> p (a b)"), s2[:, 256:384], r=["s2"], w=["W2v"], eng="pool")
    P.copy(posT.rearrange("p a b -> p (a b)"), s2[:, 384:448], r=["s2"], w=["posT"], eng="pool")
    rawv = KVraw.rearrange("p (c s) -> p s c", s=16)
    for kv in range(2):
        pr = slice(0, 64) if kv == 0 else slice(64, 128)
        for ht in range(2):
            pb = P.bank(kv)
            for l in range(32):
                P.mm(pb[:, 0:2], W1c[pr, l, ht * 128:(ht + 1) * 128], posT[pr, l, :], l == 0, l == 31,
                     r=["W1c", "posT"], w=[("b", kv)])
            P.copy(posw[:, kv * 2 + ht:kv * 2 + ht + 1], pb[:, 0:1], r=[("b", kv)], w=["posw"], eng="dve")
            for cch in range(2):
                pb2 = P.bank(2 + 2 * kv + cch)
                pk2 = ("b", 2 + 2 * kv + cch)
                for l in range(32):
                    if l < 16:
                        rhs = rawv[pr, l, cch * 512:cch * 512 + 512]
                    else:
                        rhs = rawv[pr, l - 16, cch * 512 + 1:cch * 512 + 513]
                    P.mm(pb2, W1c[pr, l, ht * 128:(ht + 1) * 128], rhs, l == 0, l == 31, r=["W1c"], w=[pk2])
                P.act(hid[kv][:, ht, cch * 512:(cch + 1) * 512], pb2, AF.Silu, r=[pk2, "posw"], w=[("hid", kv)],
                      bias=posw[:, kv * 2 + ht:kv * 2 + ht + 1], scale=1.0)
    if DBG.get("kvp_stage", 9) < 2:
        return
    for cch in range(2):
        for ht in range(2):
            P.mm(P.bank(6), W2k[:, ht, :], hid[0][:, ht, cch * 512:(cch + 1) * 512], ht == 0, ht == 1,
                 r=["W2k", ("hid", 0)], w=[("b", 6)])
        P.act(sq, P.bank(6), AF.Square, r=[("b", 6)], w=["n_sq"])
        P.mm(P.bank(7), N["blk64"], sq, True, True, r=["n_sq"], w=[("b", 7)])
        P.act(rs, P.bank(7), AF.Sqrt, r=[("b", 7)], w=["n_rs"], bias=C.eps[:, 0:1], scale=1.0)
        P.op("dve", lambda e: e.reciprocal(rs, rs), r=["n_rs"], w=["n_rs"])
        P.stt(KV["KwinT"][64:128, cch * 512:(cch + 1) * 512], P.bank(6)[64:128, :], knorm[64:128, 1:2],
              rs[64:128, :], ALU.mult, ALU.mult, r=[("b", 6), "n_rs"], w=["Kc"])
    for ct in range(8):
        b = ct % 2
        for ht in range(2):
            P.mm(P.bank(b)[:, 0:64], hid[1][:, ht, ct * 128:(ct + 1) * 128], W2v[:, ht, :], ht == 0, ht == 1,
                 r=[("hid", 1), "W2v"], w=[("b", b)])
        P.copy(KV["Vc"][:, ct, :], P.bank(b)[:, 0:64], r=[("b", b)], w=["Vc"], eng="act")
    if DBG.get("kvp_stage", 9) < 3:
        return
    P.store("sp", T["o_Ksel"], KV["KselE"][64:128, :], r=["Ksel"])
    P.store("sp", T["o_Kwin"], KV["KwinT"][0:64, :], r=["Kwin"])
    P.store("sp", T["o_Kc"], KV["KwinT"][64:128, 0:1024], r=["Kc"])
    P.store("sp", T["o_Vsel"], KV["Vsel"], r=["Vsel"])
    P.store("sp", T["o_Vwin"], KV["Vwin"], r=["Vwin"])
    P.store("sp", T["o_Vc"], KV["Vc"], r=["Vc"])


def ph_kvload(P, T, KV):
    P.load("sp", KV["KselE"][64:128, :], T["i_Ksel"], w=["Ksel"])
    P.load("sp", KV["KwinT"][0:64, :], T["i_Kwin"], w=["Kwin"])
    P.load("sp", KV["KwinT"][64:128, 0:1024], T["i_Kc"], w=["Kc"])
    P.load("act", KV["Vsel"], T["i_Vsel"], w=["Vsel"])
    P.load("act", KV["Vwin"], T["i_Vwin"], w=["Vwin"])
    P.load("act", KV["Vc"], T["i_Vc"], w=["Vc"])
    P.load("act", KV["KselE"][0:64, :], T["Zpat"], w=["KselE_z"])


def ph_qp(P, C, N, T):
    Wq = P.alloc([8, 256], BF16)
    Wg = P.alloc([8, 12], BF16)
    qnorm = P.alloc([1])
    mark0 = P.top
    stage = [P.alloc([2048]), P.alloc([2048])]
    load_cast_weight(P, T["wq"], Wq, 8, 256, stage, "Wq", 256)
    load_cast_weight(P, T["wg"], Wg, 8, 12, stage, "Wg", 12)
    P.load("act", qnorm, T["qnorm"], w=["qnorm"])
    P.ts(qnorm, qnorm, 0.125, None, ALU.mult, None, r=["qnorm"], w=["qnorm"])
    P.barrier()
    P.top = mark0
    hm = [P.alloc([8, TT], BF16), P.alloc([8, TT], BF16)]
    W = alloc_norm_work(P)
    qn_b = P.alloc([TT], BF16)
    qr_b = P.alloc([TT], BF16)
    g12 = P.alloc([TT])
    hv = T["hmixT_all"].rearrange("(k p) t -> p k t", p=128)
    for tt in range(S_LEN // TT):
        c0 = tt * TT
        h_ = hm[tt % 2]
        hkey = ("hm", tt % 2)
        P.load("sp", h_, hv[:, :, c0:c0 + TT], w=[hkey])
        for pp in range(2):
            for k in range(8):
                P.mm(P.bank(0), Wq[:, k, pp * 128:(pp + 1) * 128], h_[:, k, :], k == 0, k == 7,
                     r=["Wq", hkey], w=[("b", 0)])
            head_norm_rope(P, C, N, P.bank(0), ("b", 0), qnorm[:, 0:1], "qnorm", T["cosT"], T["sinT"], c0, TT, 1, 2, W, "q")
            P.copy(qn_b, W["kn"], r=["n_kn"], w=["qn_b"], eng="pool")
            P.tt(qr_b, W["t1"], W["t2"], ALU.add, r=["n_t1", "n_t2"], w=["qr_b"])
            for gg in range(2):
                g = 2 * pp + gg
                P.store("sp", T["QnD"][:, g, c0:c0 + TT], qn_b[gg * 64:(gg + 1) * 64, :], r=["qn_b"], final=False)
                P.store("sp", T["QrD"][:, g, c0:c0 + TT], qr_b[gg * 64:(gg + 1) * 64, :], r=["qr_b"], final=False)
        for k in range(8):
            P.mm(P.bank(3)[0:12, :], Wg[:, k, :], h_[:, k, :], k == 0, k == 7, r=["Wg", hkey], w=[("b", 3)])
        P.act(g12[0:12, :], P.bank(3)[0:12, :], AF.Sigmoid, r=[("b", 3)], w=["g12"])
        P.store("sp", T["G12D"][:, c0:c0 + TT], g12[0:12, :], r=["g12"], final=False)


def ph_att(P, C, N, T, KV):
    QB = [P.alloc([4, 128], BF16), P.alloc([4, 128], BF16)]
    QE = [[P.alloc([4, 128], BF16) for _ in range(4)] for _ in range(2)]
    G12t = [P.alloc([128]), P.alloc([128])]
    Pc = [P.alloc([512], BF16) for _ in range(8)]
    Pt = [P.alloc([512], BF16) for _ in range(3)]
    m01 = P.alloc([128], BF16)
    rinv = P.alloc([512])
    Pn = P.alloc([512])
    Pg = P.alloc([8, 128])
    imp = P.alloc([256])
    imp2 = P.alloc([256])
    m8a = P.alloc([8])
    m8b = P.alloc([8])
    thr = P.alloc([1])
    selb = P.alloc([256], BF16)
    Osb = [P.alloc([512]), P.alloc([512])]
    rden = P.alloc([512])
    gate = [P.alloc([512]) for _ in range(3)]
    acc = P.alloc([512])
    tq = P.alloc([512])
    ofin = P.alloc([4, 128], BF16)
    P.memset(Osb[0], 0.0, w=["Osb0"])
    P.memset(Osb[1], 0.0, w=["Osb1"])
    psT = P.bank(5, BF16)
    pti = 0

    def g3(ap2d):
        return ap2d.rearrange("p (g q) -> p g q", g=4)

    def bc(mask):
        return mask.unsqueeze(1).broadcast_to([128, 4, 128])

    for qt in DBG.get("att_list", range(DBG.get("att_nqt", NQT))):
        t0 = 128 * qt
        par = DBG.get("att_par", qt % 2)
        nW = (2 * qt + 1) // 64 + 1
        qb = QB[par]
        qbk = ("QB", par)
        qb2 = qb.rearrange("p g q -> p (g q)")
        P.load("sp", qb[0:64, :, :], T["QrD"][:, :, t0:t0 + 128], w=[qbk])
        P.load("sp", qb[64:128, :, :], T["QnD"][:, :, t0:t0 + 128], w=[qbk])
        for w in range(nW):
            P.load("sp", QE[par][w][64:128, :, :], T["QrD"][:, :, t0:t0 + 128], w=[("QEq", par, w)])
        P.load("sp", G12t[par][0:12, :], T["G12D"][:, t0:t0 + 128], w=[("G12", par)])
        n_ct = (8 * qt + 6) // 128 + 1
        for ct in range(n_ct):
            P.mm(P.bank(0), KV["KwinT"][64:128, ct * 128:(ct + 1) * 128], qb2[64:128, :], True, True,
                 r=[qbk], w=[("b", 0)])
            P.act(Pc[ct], P.bank(0), AF.Exp, r=[("b", 0)], w=[("Pc", ct)])
            th = t0 - 2048 * ct
            if th < 2063:
                P.ts(m01, N["Dm"], float(th), None, ALU.is_le, None, r=[], w=["m01"])
                P.tt(g3(Pc[ct]), g3(Pc[ct]), bc(m01), ALU.mult, r=[("Pc", ct), "m01"], w=[("Pc", ct)])
        for ct in range(n_ct):
            P.mm(P.bank(2), N["ones_c"], Pc[ct], ct == 0, ct == n_ct - 1, r=[("Pc", ct)], w=[("b", 2)])
        for ct in range(n_ct):
            P.mm(P.bank(3)[0:64, :], KV["Vc"][:, ct, :], Pc[ct], ct == 0, ct == n_ct - 1, r=[("Pc", ct)], w=[("b", 3)])
        P.ts(rinv, P.bank(2), 1e-30, None, ALU.max, None, r=[("b", 2)], w=["rinv"])
        P.op("dve", lambda e: e.reciprocal(rinv, rinv), r=["rinv"], w=["rinv"])
        for ct in range(n_ct):
            P.tt(Pn, Pc[ct], rinv, ALU.mult, r=[("Pc", ct), "rinv"], w=["Pn"])
            P.op("dve", lambda e, ct=ct: e.tensor_reduce(Pg[:, ct, :], Pn.rearrange("p (g q) -> p q g", g=4),
                                                         mybir.AxisListType.X, ALU.add),
                 r=["Pn"], w=[("Pg", ct)])
        for ct in range(n_ct):
            P.mm(P.bank(4)[:, 0:256], Pg[:, ct, :], N["ov"][:, ct, :], ct == 0, ct == n_ct - 1,
                 r=[("Pg", ct)], w=[("b", 4)])
        P.copy(imp, P.bank(4)[:, 0:256], r=[("b", 4)], w=["imp"], eng="dve")
        ncv = 2 * qt + 2
        if ncv < 256:
            P.memset(imp[:, ncv:256], -1.0, w=["imp"])
        P.memset(imp[:, 2 * qt:2 * qt + 1], 1e9, w=["imp"])
        P.memset(imp[64:128, 2 * qt + 1:2 * qt + 2], 1e9, w=["imp"])
        P.memset(imp[0:64, 2 * qt + 1:2 * qt + 2], -1.0, w=["imp"])
        if qt >= 1:
            P.memset(imp[0:64, 2 * qt - 1:2 * qt], 1e9, w=["imp"])
        P.memset(imp[:, 0:1], 1e9, w=["imp"])
        P.op("dve", lambda e: e.max(m8a, imp), r=["imp"], w=["m8a"])
        P.op("dve", lambda e: e.match_replace(imp2, m8a, imp, -2.0), r=["imp", "m8a"], w=["imp2"])
        P.op("dve", lambda e: e.max(m8b, imp2), r=["imp2"], w=["m8b"])
        P.ts(thr, m8b[:, 7:8], 0.0, None, ALU.max, None, r=["m8b"], w=["thr"])
        P.ts(selb, imp, thr[:, 0:1], NEGB, ALU.is_lt, ALU.mult, r=["imp", "thr"], w=["selb"])
        for w in range(nW):
            P.tr(psT[0:64, 0:128], selb[:, 64 * w:64 * w + 64], N["ident"], r=["selb"], w=[("b", 5)])
            P.op("dve", lambda e, w=w, par=par: e.tensor_copy(QE[par][w][0:64, :, :],
                                                     psT[0:64, 0:128].unsqueeze(1).broadcast_to([64, 4, 128])),
                 r=[("b", 5)], w=[("QEs", par, w)])
        def sel_score(kt):
            w = kt // 32
            b = kt % 2
            P.mm(P.bank(b), KV["KselE"][:, kt * 128:(kt + 1) * 128], QE[par][w].rearrange("p g q -> p (g q)"),
                 True, True, r=[("QEq", par, w), ("QEs", par, w)], w=[("b", b)])

        sel_score(0)
        for kt in range(qt + 1):
            b = kt % 2
            if kt + 1 <= qt:
                sel_score(kt + 1)
            pt, ptk = Pt[pti % 3], ("Pt", pti % 3)
            pti += 1
            P.act(pt, P.bank(b), AF.Exp, r=[("b", b)], w=[ptk])
            if kt == qt:
                P.tt(g3(pt), g3(pt), bc(N["tri01"]), ALU.mult, r=[ptk], w=[ptk])
            P.mm(P.bank(6)[0:66, :], KV["Vsel"][:, kt, :], pt, kt == 0, kt == qt, r=[ptk], w=[("b", 6)])
        k0 = max(0, qt - 4)
        for kt in range(k0, qt + 1):
            P.mm(P.bank(1), KV["KwinT"][0:64, kt * 128:(kt + 1) * 128], qb2[0:64, :], True, True,
                 r=[qbk], w=[("b", 1)])
            pt, ptk = Pt[pti % 3], ("Pt", pti % 3)
            pti += 1
            P.act(pt, P.bank(1), AF.Exp, r=[("b", 1)], w=[ptk])
            if kt == qt:
                P.tt(g3(pt), g3(pt), bc(N["tri01"]), ALU.mult, r=[ptk], w=[ptk])
            if kt == qt - 4:
                P.tt(g3(pt), g3(pt), bc(N["low01"]), ALU.mult, r=[ptk], w=[ptk])
            P.mm(P.bank(7)[0:66, :], KV["Vwin"][:, kt, :], pt, kt == k0, kt == qt, r=[ptk], w=[("b", 7)])
        for br in range(3):
            for g in range(4):
                P.mm(P.bank(5)[0:64, g * 128:(g + 1) * 128], N["SelG"][0:12, br * 4 + g, :], G12t[par][0:12, :],
                     True, True, r=[("G12", par)], w=[("b", 5)])
            P.copy(gate[br][0:64, :], P.bank(5)[0:64, :], r=[("b", 5)], w=[("gate", br)], eng="dve")
        P.tt(acc[0:64, :], P.bank(3)[0:64, :], rinv[0:64, :], ALU.mult, r=[("b", 3), "rinv"], w=["acc"])
        P.tt(acc[0:64, :], acc[0:64, :], gate[0][0:64, :], ALU.mult, r=["acc", ("gate", 0)], w=["acc"])
        for i, bk in ((0, 6), (1, 7)):
            ok = "Osb%d" % i
            P.copy(Osb[i][0:66, :], P.bank(bk)[0:66, :], r=[("b", bk)], w=[ok], eng="dve")
            P.mm(P.bank(2)[0:64, :], N["ShiftB"], Osb[i], True, True, r=[ok], w=[("b", 2)])
            P.op("dve", lambda e: e.reciprocal(rden[0:64, :], P.bank(2)[0:64, :]), r=[("b", 2)], w=["rden"])
            P.tt(tq[0:64, :], Osb[i][0:64, :], rden[0:64, :], ALU.mult, r=[ok, "rden"], w=["tq"])
            P.tt(tq[0:64, :], tq[0:64, :], gate[1 + i][0:64, :], ALU.mult, r=["tq", ("gate", 1 + i)], w=["tq"])
            P.tt(acc[0:64, :], acc[0:64, :], tq[0:64, :], ALU.add, r=["acc", "tq"], w=["acc"])
        P.copy(ofin[0:64, :, :].rearrange("p g q -> p (g q)"), acc[0:64, :], r=["acc"], w=["ofin"], eng="dve")
        P.store("sp", T["oTD"][:, :, t0:t0 + 128], ofin[0:64, :, :], r=["ofin"])


def build_nsa(with_kvp):
    P = Prog()
    C = setup_consts(P, None)
    T = {}
    nsa_const_io(P, T)
    N = nsa_load_consts(P, T)
    KV = alloc_kv(P)
    P.set_mark()
    if with_kvp:
        T.update(hkvT_all=P.din("hkvT_all", [D, S_LEN], BF16), wkv=P.din("wkv", [D, 384]),
                 knorm=P.din("knorm", [128, 2]), w1c=P.din("w1c", [128, 32, 256]), posT2=P.din("posT2", [128, 32, 2]),
                 w2k=P.din("w2k", [128, 2, 128]), w2v=P.din("w2v", [128, 2, 64]),
                 o_Ksel=P.dout("o_Ksel", [64, S_LEN], BF16), o_Kwin=P.dout("o_Kwin", [64, S_LEN], BF16),
                 o_Kc=P.dout("o_Kc", [64, 1024], BF16), o_Vsel=P.dout("o_Vsel", [128, 128, 66], BF16),
                 o_Vwin=P.dout("o_Vwin", [128, 128, 66], BF16), o_Vc=P.dout("o_Vc", [128, 8, 64], BF16))
        if not DBG.get("skip_kvp"):
            ph_kvp(P, C, N, T, KV)
    else:
        T.update(i_Ksel=P.din("i_Ksel", [64, S_LEN], BF16), i_Kwin=P.din("i_Kwin", [64, S_LEN], BF16),
                 i_Kc=P.din("i_Kc", [64, 1024], BF16), i_Vsel=P.din("i_Vsel", [128, 128, 66], BF16),
                 i_Vwin=P.din("i_Vwin", [128, 128, 66], BF16), i_Vc=P.din("i_Vc", [128, 8, 64], BF16))
        ph_kvload(P, T, KV)
    P.new_phase()
    T.update(hmixT_all=P.din("hmixT_all", [D, S_LEN], BF16), wq=P.din("wq", [D, 256]), wg=P.din("wg", [D, 12]),
             qnorm=P.din("qnorm", [128, 1]),
             QnD=P.dscratch("QnD", [64, 4, S_LEN], BF16), QrD=P.dscratch("QrD", [64, 4, S_LEN], BF16),
             G12D=P.dscratch("G12D", [12, S_LEN]), oTD=P.dout("oTD", [64, 4, S_LEN], BF16))
    if not DBG.get("skip_qp"):
        ph_qp(P, C, N, T)
    P.new_phase()
    if not DBG.get("skip_att"):
        ph_att(P, C, N, T, KV)
    return P.build()


def build_L5(last):
    P = Prog()
    modsT, C = _start(P)
    layer = 3 if last else 2
    T = dict(xT=P.din("xT", [D, TC]), x1T=P.dscratch("x1T", [D, TC]), uT=P.din("uT", [D, TC], BF16),
             w_out=P.din("w_out", [D, D]))
    ph_oproj(P, C, T, modsT, layer, False)
    P.new_phase()
    T2 = dict(x1T=T["x1T"], x2T=P.dout("x2T", [D, TC]))
    _ffn_io(P, T2)
    extra = []
    if not last:
        extra = [(P.din("gain_mixn", [128, 8]), modsT[:, 144:152], modsT[:, 152:160], P.dout("hmixT", [D, TC], BF16))]
    ph_ffn(P, C, T2, modsT, layer, extra)
    return P.build()


def _hs_nsa_methods():
    def gather_batch(self, per_core):
        return [np.ascontiguousarray(np.concatenate([per_core[4 * b + i] for i in range(4)], axis=1)) for b in range(2)]

    def run_nsa(self, li):
        inp = self.inp
        first = (li == 0)
        nc = get_prog("NSA%d" % li, lambda: build_nsa(first))
        if not hasattr(self, "nsac"):
            self.nsac = nsa_host_consts()
        hmix_all = gather_batch(self, self.hmixT)
        if first:
            hkv_all = gather_batch(self, self.hkvT)
            wkv6 = inp["nsa_w_kv"].reshape(D, 6, 4, 64)
            w1 = inp["cmp_w1"]
            w1c = np.ascontiguousarray(np.concatenate(
                [w1[i].reshape(32, 64, 256).transpose(1, 0, 2) for i in range(2)], axis=0))
            posT2 = np.ascontiguousarray(np.repeat(np.concatenate(
                [inp["cmp_pos"][i].T for i in range(2)], axis=0)[:, :, None], 2, axis=2))
            w2k = np.zeros((128, 2, 128), np.float32)
            w2k[:, :, 64:128] = inp["cmp_w2"][0].reshape(2, 128, 64).transpose(1, 0, 2)
            w2v = np.ascontiguousarray(inp["cmp_w2"][1].reshape(2, 128, 64).transpose(1, 0, 2))
            kn = inp["nsa_k_norm"]
            knorm = np.zeros((128, 2), np.float32)
            knorm[:, 0] = np.concatenate([kn[2], kn[1]])
            knorm[64:, 1] = kn[0]
        wq_full = inp["nsa_w_q"][li]
        maps = []
        for c in range(8):
            b, r = c // 4, c % 4
            m = dict(self.nsac)
            m["hmixT_all"] = hmix_all[b]
            m["wq"] = np.ascontiguousarray(wq_full[:, 256 * r:256 * (r + 1)])
            gcols = [1024 + br * 16 + 4 * r + g for br in range(3) for g in range(4)]
            m["wg"] = np.ascontiguousarray(wq_full[:, gcols])
            m["qnorm"] = np.ascontiguousarray(np.tile(inp["nsa_q_norm"][li], 2).reshape(128, 1))
            if first:
                m["hkvT_all"] = hkv_all[b]
                m["wkv"] = np.ascontiguousarray(wkv6[:, [4, 2, 0, 1, 3, 5], r, :].reshape(D, 384))
                m.update(knorm=knorm, w1c=w1c, posT2=posT2, w2k=w2k, w2v=w2v)
            else:
                m.update({"i_" + k: v for k, v in self.kvstore[c].items()})
            maps.append(m)
        res = run_prog(nc, maps)
        if first:
            self.kvstore = [{k: np.asarray(res[c]["o_" + k]) for k in ("Ksel", "Kwin", "Kc", "Vsel", "Vwin", "Vc")}
                            for c in range(8)]
        oT = [np.asarray(res[c]["oTD"]) for c in range(8)]
        self.uT = []
        for c in range(8):
            b, rr = c // 4, c % 4
            parts = [oT[4 * b + r][:, :, rr * TC:(rr + 1) * TC].transpose(1, 0, 2).reshape(256, TC) for r in range(4)]
            self.uT.append(np.ascontiguousarray(np.concatenate(parts, axis=0)))

    def run_L5(self, last):
        inp = self.inp
        li = 1 if last else 0
        layer = 2 + li
        nc = get_prog("L5_%d" % li, lambda: build_L5(last))
        maps = []
        for c in range(8):
            m = dict(modsT_in=self.modsT[c // 4], xT=self.xT[c], uT=self.uT[c], w_out=inp["nsa_w_out"][li],
                     **self._ffn_inputs(layer))
            if not last:
                m["gain_mixn"] = fm(inp["norm_mix"][3])
            maps.append(m)
        res = run_prog(nc, maps)
        self.xT = [np.asarray(res[c]["x2T"]) for c in range(8)]
        if not last:
            self.hmixT = [np.asarray(res[c]["hmixT"]) for c in range(8)]

    HostState.run_nsa = run_nsa
    HostState.run_L5 = run_L5


_hs_nsa_methods()


def kernel(**inputs):
    st = HostState(inputs)
    st.run_L0()
    st.run_L1()
    st.run_L2()
    st.run_L3()
    st.run_nsa(0)
    st.run_L5(False)
    st.run_nsa(1)
    st.run_L5(True)
    out = np.stack([np.concatenate([st.xT[4 * b + r].T for r in range(4)], axis=0) for b in range(2)])
    return np.ascontiguousarray(out.astype(np.float32))
```
